# Optimizing a Trainium2 kernel written in Bass

```python
import math
import jax, jax.numpy as jnp
from jax import lax
import numpy as np

D_MODEL = 1024
BATCH = 8
SEQ = 8192
DEPTH = 2
DEC_BATCH = 8
DEC_SEQ = 2048
PAST_LEN = 128

GRID_W = 64
MIX_WIDTH = D_MODEL
S5_WIDTH = MIX_WIDTH // 2
S5_GROUP_CH = 16
S5_GROUPS = S5_WIDTH // S5_GROUP_CH
S5_STATE = 64
DT_MIN = 1e-3
DT_MAX = 1e-1
N_Q_HEADS = 8
N_KV_HEADS = 2
HEAD_DIM = 64
Q_GROUP = N_Q_HEADS // N_KV_HEADS
ATTN_WIDTH = N_Q_HEADS * HEAD_DIM
KV_WIDTH = N_KV_HEADS * HEAD_DIM
IN_WIDTH = S5_WIDTH + ATTN_WIDTH + 2 * KV_WIDTH
Q_BLOCK = 128
ROPE_THETA = 10000.0
ROPE_AXIS_DIM = HEAD_DIM // 2
POOL_WINDOWS = (2, 4, 8, 16)
POOL_GROUP_CH = D_MODEL // len(POOL_WINDOWS)
D_FF = 2816
CONV_WIDTH = 3
EPS = 1e-6
N_EVEN = (DEPTH + 1) // 2
N_ODD = DEPTH // 2

kernel_name = "hybrid_s5_gqa_pool_encoder"


def _rms_norm(x, g):
    xf = x.astype(jnp.float32)
    y = xf * lax.rsqrt(jnp.mean(xf * xf, axis=-1, keepdims=True) + EPS)
    return (y * g.astype(jnp.float32)).astype(x.dtype)


def _complex_affine_combine(e1, e2):
    a1r, a1i, b1r, b1i = e1
    a2r, a2i, b2r, b2i = e2
    return (a2r * a1r - a2i * a1i,
            a2r * a1i + a2i * a1r,
            a2r * b1r - a2i * b1i + b2r,
            a2r * b1i + a2i * b1r + b2i)


def _s5_mixer(u, lam_re, lam_im, log_dt, b_re, b_im, c_re, c_im, d, w_glu, b_glu):
    f32 = jnp.float32
    bsz, L, _ = u.shape
    uf = u.astype(f32).reshape(bsz, L, S5_GROUPS, S5_GROUP_CH)
    y = d.astype(f32).reshape(S5_GROUPS, S5_GROUP_CH) * uf
    a_shape = (1, L, S5_GROUPS, S5_STATE)
    for direction in range(2):
        lr = lam_re[direction].astype(f32)
        li = lam_im[direction].astype(f32)
        dt = jnp.exp(log_dt[direction].astype(f32))[:, None]
        mag = jnp.exp(lr * dt)
        ar = mag * jnp.cos(li * dt)
        ai = mag * jnp.sin(li * dt)
        den = lr * lr + li * li
        zr = ((ar - 1.0) * lr + ai * li) / den
        zi = (ai * lr - (ar - 1.0) * li) / den
        br = b_re[direction].astype(f32)
        bi = b_im[direction].astype(f32)
        bbar_r = zr[..., None] * br - zi[..., None] * bi
        bbar_i = zr[..., None] * bi + zi[..., None] * br
        xr = jnp.einsum('blgc,gpc->blgp', uf, bbar_r)
        xi = jnp.einsum('blgc,gpc->blgp', uf, bbar_i)
        _, _, sr, si = lax.associative_scan(
            _complex_affine_combine,
            (jnp.broadcast_to(ar, a_shape), jnp.broadcast_to(ai, a_shape), xr, xi),
            reverse=(direction == 1), axis=1)
        y = (y + jnp.einsum('blgp,gcp->blgc', sr, c_re[direction].astype(f32))
             - jnp.einsum('blgp,gcp->blgc', si, c_im[direction].astype(f32)))
    y = jax.nn.gelu(y.reshape(bsz, L, S5_WIDTH))
    gate = jax.nn.sigmoid(y @ w_glu.astype(f32) + b_glu.astype(f32))
    return (y * gate).astype(u.dtype)


def _axial_rope_tables(L):
    rows = L // GRID_W
    row = jnp.repeat(jnp.arange(rows, dtype=jnp.float32), GRID_W)
    col = jnp.tile(jnp.arange(GRID_W, dtype=jnp.float32), rows)
    n_freq = ROPE_AXIS_DIM // 2
    inv_freq = jnp.power(ROPE_THETA, -jnp.arange(n_freq, dtype=jnp.float32) / n_freq)
    ang_r = row[:, None] * inv_freq
    ang_c = col[:, None] * inv_freq
    return (jnp.cos(ang_r), jnp.sin(ang_r), jnp.cos(ang_c), jnp.sin(ang_c))


def _rotate(x, cos, sin):
    half = x.shape[-1] // 2
    x1, x2 = x[..., :half], x[..., half:]
    c = cos[None, :, None, :]
    s = sin[None, :, None, :]
    return jnp.concatenate([x1 * c - x2 * s, x1 * s + x2 * c], axis=-1)


def _apply_axial_rope(x, rope):
    cr, sr, cc, sc = rope
    xf = x.astype(jnp.float32)
    out = jnp.concatenate([_rotate(xf[..., :ROPE_AXIS_DIM], cr, sr),
                           _rotate(xf[..., ROPE_AXIS_DIM:], cc, sc)], axis=-1)
    return out.astype(x.dtype)


def _block_attention(q, k, v):
    bsz, L, _, _ = q.shape
    nblk = L // Q_BLOCK
    qb = q.reshape(bsz, nblk, Q_BLOCK, N_KV_HEADS, Q_GROUP, HEAD_DIM).transpose(1, 0, 2, 3, 4, 5)
    scale = HEAD_DIM ** -0.5

    def one_block(qblk):
        s = jnp.einsum('bqhgd,bkhd->bhgqk', qblk, k).astype(jnp.float32) * scale
        p = jax.nn.softmax(s, axis=-1).astype(v.dtype)
        return jnp.einsum('bhgqk,bkhd->bqhgd', p, v)

    out = lax.map(one_block, qb)
    return out.transpose(1, 0, 2, 3, 4, 5).reshape(bsz, L, ATTN_WIDTH)


def _pool_mixer(h, w_pool, scale):
    f32 = jnp.float32
    bsz, L, _ = h.shape
    hf = h.astype(f32)
    cs = jnp.concatenate([jnp.zeros((bsz, 1, D_MODEL), f32), jnp.cumsum(hf, axis=1)], axis=1)
    t = np.arange(L)
    outs = []
    for g, w in enumerate(POOL_WINDOWS):
        lo = np.maximum(t - w // 2, 0)
        hi = np.minimum(t + (w - w // 2) - 1, L - 1)
        cnt = jnp.asarray((hi - lo + 1).astype(np.float32))
        sl = slice(g * POOL_GROUP_CH, (g + 1) * POOL_GROUP_CH)
        csg = cs[..., sl]
        mean = (csg[:, hi + 1] - csg[:, lo]) / cnt[None, :, None]
        outs.append(jnp.einsum('blc,cd->bld', mean - hf[..., sl], w_pool[g].astype(f32)))
    return (jnp.concatenate(outs, axis=-1) * scale.astype(f32)).astype(h.dtype)


def _conv_glu_ffn(h, w_up, conv_w, conv_b, w_down):
    up = h @ w_up
    kern = conv_w[:, None, :].astype(up.dtype)
    up = lax.conv_general_dilated(up, kern, window_strides=(1,), padding='SAME',
                                  dimension_numbers=('NWC', 'WIO', 'NWC'),
                                  feature_group_count=2 * D_FF) + conv_b.astype(up.dtype)
    g, val = jnp.split(up, 2, axis=-1)
    return (jax.nn.silu(g) * val) @ w_down


def _trunk(x, mix_norm, ffn_norm, final_norm, w_in, s5_lambda_re, s5_lambda_im, s5_log_dt,
           s5_b_re, s5_b_im, s5_c_re, s5_c_im, s5_d, s5_w_glu, s5_b_glu, q_norm, k_norm,
           w_out, pool_w, pool_scale, ffn_w_up, ffn_conv_w, ffn_conv_b, ffn_w_down):
    bsz, L, _ = x.shape
    rope = _axial_rope_tables(L)
    splits = [S5_WIDTH, S5_WIDTH + ATTN_WIDTH, S5_WIDTH + ATTN_WIDTH + KV_WIDTH]
    for layer in range(DEPTH):
        h = _rms_norm(x, mix_norm[layer])
        if layer % 2 == 0:
            e = layer // 2
            proj = h @ w_in[e]
            u, q, k, v = jnp.split(proj, splits, axis=-1)
            s5_out = _s5_mixer(u, s5_lambda_re[e], s5_lambda_im[e], s5_log_dt[e], s5_b_re[e],
                               s5_b_im[e], s5_c_re[e], s5_c_im[e], s5_d[e], s5_w_glu[e], s5_b_glu[e])
            q = _apply_axial_rope(_rms_norm(q.reshape(bsz, L, N_Q_HEADS, HEAD_DIM), q_norm[e]), rope)
            k = _apply_axial_rope(_rms_norm(k.reshape(bsz, L, N_KV_HEADS, HEAD_DIM), k_norm[e]), rope)
            v = v.reshape(bsz, L, N_KV_HEADS, HEAD_DIM)
            attn_out = _block_attention(q, k, v)
            mixed = jnp.concatenate([s5_out, attn_out.astype(s5_out.dtype)], axis=-1) @ w_out[e]
        else:
            o = layer // 2
            mixed = _pool_mixer(h, pool_w[o], pool_scale[o])
        x = x + mixed.astype(x.dtype)
        h = _rms_norm(x, ffn_norm[layer])
        x = x + _conv_glu_ffn(h, ffn_w_up[layer], ffn_conv_w[layer], ffn_conv_b[layer],
                              ffn_w_down[layer]).astype(x.dtype)
    return _rms_norm(x, final_norm)


def setup_inputs(seed: int = 0) -> dict:
    key = jax.random.key(seed)
    ks = jax.random.split(key, 32)
    f32 = jnp.float32

    def nrm(k, shape, scale):
        return jax.random.normal(k, shape, f32) * scale

    lam_shape = (N_EVEN, 2, S5_GROUPS, S5_STATE)
    b_shape = (N_EVEN, 2, S5_GROUPS, S5_STATE, S5_GROUP_CH)
    c_shape = (N_EVEN, 2, S5_GROUPS, S5_GROUP_CH, S5_STATE)
    return {
        "x_prompt": nrm(ks[0], (BATCH, SEQ, D_MODEL), 1.0),
        "x_sample": nrm(ks[1], (DEC_BATCH, DEC_SEQ, D_MODEL), 1.0),
        "mix_norm": 1.0 + nrm(ks[2], (DEPTH, D_MODEL), 0.02),
        "ffn_norm": 1.0 + nrm(ks[3], (DEPTH, D_MODEL), 0.02),
        "final_norm": 1.0 + nrm(ks[4], (D_MODEL,), 0.02),
        "w_in": nrm(ks[5], (N_EVEN, D_MODEL, IN_WIDTH), D_MODEL ** -0.5),
        "s5_lambda_re": -0.5 + nrm(ks[6], lam_shape, 0.01),
        "s5_lambda_im": jnp.pi * jnp.arange(S5_STATE, dtype=f32) + nrm(ks[7], lam_shape, 0.01),
        "s5_log_dt": jax.random.uniform(ks[8], (N_EVEN, 2, S5_GROUPS), f32,
                                        math.log(DT_MIN), math.log(DT_MAX)),
        "s5_b_re": nrm(ks[9], b_shape, (2 * S5_GROUP_CH) ** -0.5),
        "s5_b_im": nrm(ks[10], b_shape, (2 * S5_GROUP_CH) ** -0.5),
        "s5_c_re": nrm(ks[11], c_shape, S5_STATE ** -0.5),
        "s5_c_im": nrm(ks[12], c_shape, S5_STATE ** -0.5),
        "s5_d": nrm(ks[13], (N_EVEN, S5_WIDTH), 1.0),
        "s5_w_glu": nrm(ks[14], (N_EVEN, S5_WIDTH, S5_WIDTH), S5_WIDTH ** -0.5),
        "s5_b_glu": nrm(ks[15], (N_EVEN, S5_WIDTH), 0.01),
        "q_norm": 1.0 + nrm(ks[16], (N_EVEN, HEAD_DIM), 0.02),
        "k_norm": 1.0 + nrm(ks[17], (N_EVEN, HEAD_DIM), 0.02),
        "w_out": nrm(ks[18], (N_EVEN, MIX_WIDTH, D_MODEL), MIX_WIDTH ** -0.5),
        "pool_w": nrm(ks[19], (N_ODD, len(POOL_WINDOWS), POOL_GROUP_CH, POOL_GROUP_CH), POOL_GROUP_CH ** -0.5),
        "pool_scale": 1.0 + nrm(ks[20], (N_ODD, D_MODEL), 0.02),
        "ffn_w_up": nrm(ks[21], (DEPTH, D_MODEL, 2 * D_FF), D_MODEL ** -0.5),
        "ffn_conv_w": nrm(ks[22], (DEPTH, CONV_WIDTH, 2 * D_FF), CONV_WIDTH ** -0.5),
        "ffn_conv_b": nrm(ks[23], (DEPTH, 2 * D_FF), 0.01),
        "ffn_w_down": nrm(ks[24], (DEPTH, D_FF, D_MODEL), D_FF ** -0.5),
    }


def reference(x_prompt, x_sample, mix_norm, ffn_norm, final_norm, w_in, s5_lambda_re, s5_lambda_im,
              s5_log_dt, s5_b_re, s5_b_im, s5_c_re, s5_c_im, s5_d, s5_w_glu, s5_b_glu, q_norm, k_norm,
              w_out, pool_w, pool_scale, ffn_w_up, ffn_conv_w, ffn_conv_b, ffn_w_down):
    params = (mix_norm, ffn_norm, final_norm, w_in, s5_lambda_re, s5_lambda_im, s5_log_dt,
              s5_b_re, s5_b_im, s5_c_re, s5_c_im, s5_d, s5_w_glu, s5_b_glu, q_norm, k_norm,
              w_out, pool_w, pool_scale, ffn_w_up, ffn_conv_w, ffn_conv_b, ffn_w_down)
    y_prompt = _trunk(x_prompt, *params)
    y_sample = _trunk(x_sample, *params)
    return (y_prompt, y_sample)
```

```python
from contextlib import ExitStack
import numpy as np
import concourse.bass as bass
import concourse.mybir as mybir
from concourse.bass_utils import run_bass_kernel_spmd

F32 = mybir.dt.float32
BF16 = mybir.dt.bfloat16
ALU = mybir.AluOpType
AF = mybir.ActivationFunctionType
AX = mybir.AxisListType

import os
S5CUT = int(os.environ.get("S5CUT", "0"))
D = 1024
FF = 2816
NFT = FF // 128
EPS = 1e-6
NDMA_SEM = 12


class Tl:
    def __init__(self, t, name):
        self.t = t
        self.name = name
        self.writes = {}
        self.reads = {}

    def __getitem__(self, idx):
        return self.t[idx]


class _Rec:
    def __getattr__(self, name):
        def f(*args, **kw):
            self.call = (name, args, kw)
        return f


class Sched:
    def __init__(self, nc, es):
        self.nc = nc
        self.es = es
        self.names = ["pe", "act", "dve", "pool", "sp"]
        self.prog = {e: [] for e in self.names}
        self.sem = {e: es.enter_context(nc.semaphore("s_" + e)) for e in self.names}
        self.cnt = {e: 0 for e in self.names}
        self.seen = {e: {} for e in self.names}
        self.dsem = {}
        self.dval = {}
        self.drr = {}
        for q in ("sp", "pool", "act"):
            self.dsem[q] = [es.enter_context(nc.semaphore("d_%s%d" % (q, i))) for i in range(NDMA_SEM)]
            self.dval[q] = [0] * NDMA_SEM
            self.drr[q] = 0
        self.ntile = 0

    def sb(self, shape, dt, name=None):
        self.ntile += 1
        name = "%s_%d" % (name or "t", self.ntile)
        return Tl(self.es.enter_context(self.nc.sbuf_tensor(name, list(shape), dt)), name)

    def ps(self, shape, dt, name=None):
        self.ntile += 1
        name = "%s_%d" % (name or "p", self.ntile)
        return Tl(self.es.enter_context(self.nc.psum_tensor(name, list(shape), dt)), name)

    def _deps(self, eng, reads, writes):
        deps = {}

        def add(d):
            for k, (s, v) in d.items():
                if k not in deps or deps[k][1] < v:
                    deps[k] = (s, v)

        for t in reads:
            add(t.writes)
        for t in writes:
            add(t.writes)
            add(t.reads)
        out = []
        for k, (s, v) in deps.items():
            if eng == "pe" and k == "pe":
                continue
            if self.seen[eng].get(k, 0) >= v:
                continue
            self.seen[eng][k] = v
            out.append((s, v))
        return out

    def _mark(self, reads, writes, key, ticket):
        for t in reads:
            t.reads[key] = ticket
        for t in writes:
            t.writes = {key: ticket}
            t.reads = {}

    def op(self, eng, fn, reads=(), writes=(), inc=True):
        waits = self._deps(eng, reads, writes)
        if inc:
            self.cnt[eng] += 1
            ticket = (self.sem[eng], self.cnt[eng])
        else:
            ticket = (self.sem[eng], self.cnt[eng] + 1)
        rec = _Rec()
        fn(rec)
        name, args, kw = rec.call
        self.prog[eng].append((waits, lambda e: getattr(e, name)(*args, **kw),
                               (self.sem[eng], 1) if inc else None))
        self._mark(reads, writes, eng, ticket)

    def dma(self, q, out, in_, reads=(), writes=(), **kw):
        i = self.drr[q]
        self.drr[q] = (i + 1) % NDMA_SEM
        sem = self.dsem[q][i]
        key = "d_%s%d" % (q, i)
        waits = self._deps(q, reads, writes)
        prev = self.dval[q][i]
        if prev > 0 and self.seen[q].get(key, 0) < prev:
            self.seen[q][key] = prev
            waits.append((sem, prev))
        self.dval[q][i] = prev + 16
        self.prog[q].append((waits, lambda e: e.dma_start(out=out, in_=in_, **kw), (sem, 16)))
        self._mark(reads, writes, key, (sem, prev + 16))

    def barrier(self):
        allw = [(self.sem[e], self.cnt[e], e) for e in self.names if self.cnt[e] > 0]
        for q in self.dsem:
            for i in range(NDMA_SEM):
                if self.dval[q][i] > 0:
                    allw.append((self.dsem[q][i], self.dval[q][i], "d_%s%d" % (q, i)))
        for e in self.names:
            waits = []
            for (s, v, k) in allw:
                if self.seen[e].get(k, 0) < v:
                    self.seen[e][k] = v
                    waits.append((s, v))
            if waits:
                self.prog[e].append((waits, None, None))

    def emit(self):
        nc = self.nc
        engs = {"pe": "tensor", "act": "scalar", "dve": "vector", "pool": "gpsimd", "sp": "sync"}
        with nc.Block() as block:
            for e in self.names:
                prog = self.prog[e]

                def body(eng, prog=prog):
                    for waits, fn, inc in prog:
                        for (s, v) in waits:
                            eng.wait_ge(s, v)
                        if fn is None:
                            continue
                        ins = fn(eng)
                        if inc is not None:
                            ins.then_inc(inc[0], inc[1])

                getattr(block, engs[e])(body)
        self.prog = {e: [] for e in self.names}

    def run_phase(self, fn):
        outer = self.es
        with ExitStack() as pes:
            self.es = pes
            fn()
            self.barrier()
            self.emit()
        self.es = outer


def build_program(nc, LP, LS, phases=("ffn0",), dbg=False):
    es = ExitStack()
    with es:
        K = Sched(nc, es)
        nc_es = es

        def din(name, shape, dt=F32):
            return nc.dram_tensor(name, list(shape), dt, kind="ExternalInput").ap()

        def dscr(name, shape, dt=F32):
            return nc.dram_tensor(name, list(shape), dt, kind="Internal").ap()

        seqs = [("p", LP), ("s", LS)]
        xin = {"p": din("x_p", [LP, D]), "s": din("x_s", [LS, D])}
        yout = {
            "p": nc.dram_tensor("y_p", [LP, D], F32, kind="ExternalOutput").ap(),
            "s": nc.dram_tensor("y_s", [LS, D], F32, kind="ExternalOutput").ap(),
        }
        ffn_norm = din("ffn_norm", [2, D])
        final_norm = din("final_norm", [D])
        w_up = din("ffn_w_up", [2, D, 2 * FF])
        conv_w = din("ffn_conv_w", [2, 3, 2 * FF])
        conv_b = din("ffn_conv_b", [2, 2 * FF])
        w_down = din("ffn_w_down", [2, FF, D])
        ident_in = din("ident", [128, 128])

        ident_f = K.sb([128, 128], F32, "ident_f")
        ident_b = K.sb([128, 128], BF16, "ident_b")
        K.dma("sp", ident_f[:], ident_in, writes=[ident_f])
        K.op("dve", lambda e: e.tensor_copy(out=ident_b[:], in_=ident_f[:]), reads=[ident_f], writes=[ident_b])

        def load_rep(vec_ap, n, name):
            t = K.sb([128, n], F32, name)
            K.dma("sp", t[:], vec_ap.partition_broadcast(128), writes=[t])
            return t

        def run_interleaved(gens, lead=None):
            gens = list(gens)
            for g, n_ in zip(gens, lead or []):
                for _ in range(n_):
                    next(g)
            while gens:
                for g in list(gens):
                    try:
                        next(g)
                    except StopIteration:
                        gens.remove(g)

        NXB = 3
        st_small = [K.sb([128, 4], F32, "st%d" % i) for i in range(NXB)]
        hb_tiles = [K.sb([128, D], BF16, "hb%d" % i) for i in range(NXB)]
        tp_hold = {}

        def tp_get():
            if tp_hold.get("es") is not K.es:
                tp_hold["es"] = K.es
                tp_hold["ps"] = [K.ps([128, 8, 128], BF16, "tp_ps%d" % i) for i in range(2)]
            return tp_hold["ps"]
        ctr = {"n": 0, "tp": 0}

        neghalf = K.sb([128, 1], F32, "neghalf")
        K.op("pool", lambda e: e.memset(neghalf[:, :], -0.5), writes=[neghalf])

        def rstd_ops(st, rows, n=D):
            K.op("pool", lambda e: e.tensor_scalar(out=st[:rows, 1:2], in0=st[:rows, 0:1], scalar1=1.0 / n,
                                                   scalar2=EPS, op0=ALU.mult, op1=ALU.add),
                 reads=[st], writes=[st])
            K.op("pool", lambda e: e.tensor_tensor(out=st[:rows, 2:3], in0=st[:rows, 1:2], in1=neghalf[:rows, 0:1],
                                                   op=ALU.pow), reads=[st, neghalf], writes=[st])

        def norm_part(x_t, rows, gamma_rep):
            i = ctr["n"] % NXB
            ctr["n"] += 1
            st = st_small[i]
            hb = hb_tiles[i]
            K.op("act", lambda e: e.activation(out=hb[:rows, :], in_=x_t[:rows, :], func=AF.Square,
                                               accum_out=st[:rows, 0:1]),
                 reads=[x_t], writes=[hb, st])
            rstd_ops(st, rows)
            K.op("dve", lambda e: e.scalar_tensor_tensor(out=hb[:rows, :], in0=x_t[:rows, :],
                                                         scalar=st[:rows, 2:3], in1=gamma_rep[:rows, :],
                                                         op0=ALU.mult, op1=ALU.mult),
                 reads=[x_t, st, gamma_rep], writes=[hb])
            return hb

        def T_part(hb, rows, dst_tile, dst_ap):
            j = ctr["tp"] % 2
            ctr["tp"] += 1
            tp = tp_get()[j]
            for k in range(8):
                K.op("pe", lambda e, k=k: e.transpose(out=tp[:, k, :rows], in_=hb[:rows, k * 128:(k + 1) * 128],
                                                      identity=ident_b[:rows, :rows]),
                     reads=[hb, ident_b], writes=[tp], inc=(k == 7))
            K.op("act", lambda e: e.copy(out=dst_ap, in_=tp[:, :, :rows]), reads=[tp], writes=[dst_tile])

        def norm_to_T(x_t, rows, gamma_rep, dst_tile, dst_ap):
            T_part(norm_part(x_t, rows, gamma_rep), rows, dst_tile, dst_ap)

        def ffn_phase(layer, src, dst, final):
            TB = 256
            wup = K.sb([128, 8, 2 * FF], BF16, "wup")
            wdn = K.sb([128, NFT, D], BF16, "wdn")
            for k in range(8):
                K.dma("pool", wup[:, k, :], w_up[layer, k * 128:(k + 1) * 128, :], writes=[wup])
            for j in range(NFT):
                K.dma("pool", wdn[:, j, :], w_down[layer, j * 128:(j + 1) * 128, :], writes=[wdn])
            cw = K.sb([128, 44, 3], F32, "cw")
            cb = K.sb([128, 44], F32, "cb")
            for d3 in range(3):
                K.dma("sp", cw[:, :, d3], conv_w[layer, d3, :].rearrange("(j p) -> p j", p=128), writes=[cw],
                      allow_slow_non_contiguous=True)
            K.dma("sp", cb[:], conv_b[layer, :].rearrange("(j p) -> p j", p=128), writes=[cb],
                  allow_slow_non_contiguous=True)
            gam = load_rep(ffn_norm[layer, :], D, "gam_ffn")
            gfin = load_rep(final_norm, D, "gam_fin") if final else None

            NB = 2
            xt = [[K.sb([128, D], F32, "fx%d_%d" % (b, i)) for i in range(2)] for b in range(NB)]
            hT = [K.sb([128, 8, TB + 2], BF16, "fhT%d" % b) for b in range(NB)]
            actT = [K.sb([128, NFT, TB], BF16, "factT%d" % b) for b in range(NB)]
            up_ps = [K.ps([128, 512], F32, "up_ps%d" % i) for i in range(4)]
            dn_ps = [K.ps([128, 512], F32, "dn_ps%d" % i) for i in range(2)]
            cg = [K.sb([128, TB], F32, "cg%d" % i) for i in range(2)]
            cv = [K.sb([128, TB], F32, "cv%d" % i) for i in range(2)]
            sg = [K.sb([128, TB], F32, "sg%d" % i) for i in range(2)]
            xo = [K.sb([128, D], F32, "fxo%d" % i) for i in range(2)]
            xh = [xo[0]] * NB
            yo = None
            fst = [K.sb([128, 4], F32, "fst%d" % i) for i in range(2)] if final else None
            cnt = {"blk": 0, "up": 0, "dn": 0, "c": 0, "xo": 0}

            blocks = [(sname, L, t0) for sname, L in seqs for t0 in range(0, L, TB)]

            pend_T = {}

            def prep_norm(bi):
                sname, L, t0 = blocks[bi]
                xs = src[sname]
                b = bi % NB
                for i in range(2):
                    K.dma("sp", xt[b][i][:], xs[t0 + i * 128:t0 + (i + 1) * 128, :], writes=[xt[b][i]])
                hTb = hT[b]
                todo = []
                lo_ok = t0 > 0
                hi_ok = t0 + TB < L
                if lo_ok and hi_ok:
                    K.dma("sp", xh[b][0:1, :], xs[t0 - 1:t0, :], writes=[xh[b]])
                    K.dma("sp", xh[b][1:2, :], xs[t0 + TB:t0 + TB + 1, :], writes=[xh[b]])
                    todo.append((norm_part(xh[b], 2, gam), 2, hTb[:, :, bass.ds(0, 2, step=TB + 1)]))
                else:
                    K.op("pool", lambda e, hTb=hTb: e.memset(hTb[:, :, bass.ds(0, 2, step=TB + 1)], 0.0),
                         writes=[hTb])
                    if lo_ok:
                        K.dma("sp", xh[b][0:1, :], xs[t0 - 1:t0, :], writes=[xh[b]])
                        todo.append((norm_part(xh[b], 1, gam), 1, hTb[:, :, 0:1]))
                    if hi_ok:
                        K.dma("sp", xh[b][0:1, :], xs[t0 + TB:t0 + TB + 1, :], writes=[xh[b]])
                        todo.append((norm_part(xh[b], 1, gam), 1, hTb[:, :, TB + 1:TB + 2]))
                for i in range(2):
                    todo.append((norm_part(xt[b][i], 128, gam), 128, hTb[:, :, 1 + i * 128:1 + (i + 1) * 128]))
                pend_T[bi] = todo

            def prep_T(bi):
                hTb = hT[bi % NB]
                for hb, rows, dst in pend_T.pop(bi):
                    T_part(hb, rows, hTb, dst)

            def prep(bi):
                prep_norm(bi)
                prep_T(bi)

            def up_part(bi, j0, j1):
                sname, L, t0 = blocks[bi]
                xd = dst[sname]
                b = bi % NB
                hTb = hT[b]
                aT = actT[b]
                for j in range(j0, j1):
                    pss = []
                    for half in range(2):
                        ct = j + half * NFT
                        ps = up_ps[cnt["up"] % 4]
                        cnt["up"] += 1
                        for k in range(8):
                            K.op("pe", lambda e, ps=ps, k=k, ct=ct: e.matmul(
                                out=ps[:, 0:TB + 2], lhsT=wup[:, k, ct * 128:(ct + 1) * 128],
                                rhs=hTb[:, k, :], start=(k == 0), stop=(k == 7)),
                                reads=[wup, hTb], writes=[ps], inc=(k == 7))
                        pss.append((ps, ct))
                    ci = cnt["c"] % 2
                    cnt["c"] += 1
                    outs = []
                    for (ps, ct), ctile in zip(pss, (cg[ci], cv[ci])):
                        K.op("act", lambda e, ps=ps, ct=ct, ctile=ctile: e.activation(
                            out=ctile[:], in_=ps[:, 1:TB + 1], func=AF.Identity,
                            bias=cb[:, ct:ct + 1], scale=cw[:, ct, 1:2]),
                            reads=[ps, cb, cw], writes=[ctile])
                        K.op("dve", lambda e, ps=ps, ct=ct, ctile=ctile: e.scalar_tensor_tensor(
                            out=ctile[:], in0=ps[:, 0:TB], scalar=cw[:, ct, 0:1], in1=ctile[:],
                            op0=ALU.mult, op1=ALU.add),
                            reads=[ps, cw, ctile], writes=[ctile])
                        K.op("dve", lambda e, ps=ps, ct=ct, ctile=ctile: e.scalar_tensor_tensor(
                            out=ctile[:], in0=ps[:, 2:TB + 2], scalar=cw[:, ct, 2:3], in1=ctile[:],
                            op0=ALU.mult, op1=ALU.add),
                            reads=[ps, cw, ctile], writes=[ctile])
                    sgt = sg[ci]
                    K.op("act", lambda e, ci=ci, sgt=sgt: e.activation(out=sgt[:], in_=cg[ci][:], func=AF.Silu),
                         reads=[cg[ci]], writes=[sgt])
                    K.op("pool", lambda e, ci=ci, sgt=sgt, j=j, aT=aT: e.tensor_tensor(
                        out=aT[:, j, :], in0=sgt[:], in1=cv[ci][:], op=ALU.mult),
                        reads=[sgt, cv[ci]], writes=[aT])
            def down_part(bi):
                sname, L, t0 = blocks[bi]
                xd = dst[sname]
                b = bi % NB
                aT = actT[b]
                for i in range(2):
                    xoi = xo[cnt["xo"] % 2]
                    yoi = xt[b][i] if final else None
                    fsti = fst[cnt["xo"] % 2] if final else None
                    cnt["xo"] += 1
                    for nh in range(2):
                        ps = dn_ps[cnt["dn"] % 2]
                        cnt["dn"] += 1
                        for j in range(NFT):
                            K.op("pe", lambda e, ps=ps, j=j, i=i, nh=nh: e.matmul(
                                out=ps[:, :], lhsT=aT[:, j, i * 128:(i + 1) * 128],
                                rhs=wdn[:, j, nh * 512:(nh + 1) * 512], start=(j == 0), stop=(j == NFT - 1)),
                                reads=[aT, wdn], writes=[ps], inc=(j == NFT - 1))
                        K.op("dve", lambda e, ps=ps, i=i, nh=nh, xoi=xoi: e.tensor_tensor(
                            out=xoi[:, nh * 512:(nh + 1) * 512], in0=ps[:, :],
                            in1=xt[b][i][:, nh * 512:(nh + 1) * 512], op=ALU.add),
                            reads=[ps, xt[b][i]], writes=[xoi])
                    if not final:
                        K.dma("pool", xd[t0 + i * 128:t0 + (i + 1) * 128, :], xoi[:], reads=[xoi])
                    else:
                        K.op("act", lambda e, xoi=xoi, fsti=fsti: e.activation(
                            out=yoi[:, :], in_=xoi[:, :], func=AF.Square, accum_out=fsti[:, 0:1]),
                            reads=[xoi], writes=[yoi, fsti])
                        rstd_ops(fsti, 128)
                        K.op("dve", lambda e, xoi=xoi, yoi=yoi, fsti=fsti: e.scalar_tensor_tensor(
                            out=yoi[:, :], in0=xoi[:, :], scalar=fsti[:, 2:3], in1=gfin[:, :],
                            op0=ALU.mult, op1=ALU.mult), reads=[xoi, fsti, gfin], writes=[yoi])
                        K.dma("pool", xd[t0 + i * 128:t0 + (i + 1) * 128, :], yoi[:], reads=[yoi])


            prep(0)
            JS, JS2 = 4, 14
            for bi in range(len(blocks)):
                up_part(bi, 0, JS)
                if bi > 0:
                    down_part(bi - 1)
                if bi + 1 < len(blocks):
                    prep_norm(bi + 1)
                up_part(bi, JS, JS2)
                if bi + 1 < len(blocks):
                    prep_T(bi + 1)
                up_part(bi, JS2, NFT)
            down_part(len(blocks) - 1)

        mix_norm = din("mix_norm", [2, D])
        w_in = din("w_in", [1, D, 1280])
        q_norm = din("q_norm", [1, 64])
        k_norm = din("k_norm", [1, 64])
        w_out = din("w_out", [1, D, D])
        pool_w = din("pool_w", [1, 4, 256, 256])
        pool_scale = din("pool_scale", [1, D])
        rope_cos = din("rope_cos", [LP, 320])
        rope_sin = din("rope_sin", [LP, 320])
        bands_in = din("bands", [4, 5, 128, 128])
        x1 = {s: dscr("x1_" + s, [L, D]) for s, L in seqs}
        x2 = {s: dscr("x2_" + s, [L, D]) for s, L in seqs}
        x3 = {s: dscr("x3_" + s, [L, D]) for s, L in seqs}
        u_d = {s: dscr("u_" + s, [L, 512], BF16) for s, L in seqs}
        if dbg:
            s5o_d = {s: nc.dram_tensor("s5o_" + s, [L, 512], BF16, kind="ExternalOutput").ap() for s, L in seqs}
        else:
            s5o_d = {s: dscr("s5o_" + s, [L, 512], BF16) for s, L in seqs}
        qT_d = {s: dscr("qT_" + s, [4, 128, L], BF16) for s, L in seqs}
        kT_d = {s: dscr("kT_" + s, [128, L], BF16) for s, L in seqs}
        v_d = {s: dscr("v_" + s, [L, 128], BF16) for s, L in seqs}
        aT_d = {s: dscr("aT_" + s, [512, L], BF16) for s, L in seqs}
        den2_d = {s: dscr("den_" + s, [8, L], F32) for s, L in seqs}

        def proj_phase():
            win = K.sb([128, 8, 1280], BF16, "win")
            for k in range(8):
                K.dma("pool", win[:, k, 0:512], w_in[0, k * 128:(k + 1) * 128, 0:512], writes=[win])
                K.dma("pool", win[:, k, 1024:1280], w_in[0, k * 128:(k + 1) * 128, 1024:1280], writes=[win])
            for h in range(8):
                slot = 2 * h if h < 4 else 2 * (h - 4) + 1
                K.dma("pool", win[:, :, 512 + slot * 64:512 + (slot + 1) * 64],
                      w_in[0, :, 512 + h * 64:512 + (h + 1) * 64].rearrange("(k p) d -> p k d", p=128), writes=[win])
            gam = load_rep(mix_norm[0, :], D, "gam_mix0")
            gain = K.sb([128, 10, 64], F32, "gain10")
            for h in range(10):
                K.dma("sp", gain[:, h, :], (q_norm if h < 8 else k_norm)[0, :].partition_broadcast(128),
                      writes=[gain])
            xa = [K.sb([128, D], F32, "pa_x%d" % i) for i in range(2)]
            hTa = [K.sb([128, 8, 128], BF16, "pa_hT%d" % i) for i in range(2)]
            pp_u = [K.ps([128, 512], F32, "pa_pu%d" % i) for i in range(2)]
            pp_q = [K.ps([128, 512], F32, "pa_pq%d" % i) for i in range(2)]
            kvb = K.ps([128, 512], F32, "pa_pkv")
            kv_h = [kvb, Tl(kvb.t, "pa_pkv_b")]
            tq = K.ps([128, 5, 128], BF16, "pa_tq")
            sqt_l = [K.sb([128, 640], F32, "pa_sq%d" % i) for i in range(2)]
            ss_l = [K.sb([128, 40], F32, "pa_ss%d" % i) for i in range(2)]
            qk_l = [K.sb([128, 640], F32, "pa_qk%d" % i) for i in range(2)]
            qkr_l = [K.sb([128, 640], BF16, "pa_qkr%d" % i) for i in range(2)]
            t1_l = [K.sb([128, 320], F32, "pa_t1%d" % i) for i in range(2)]
            t2_l = [K.sb([128, 320], F32, "pa_t2%d" % i) for i in range(2)]
            cs = [K.sb([128, 320], F32, "pa_cos%d" % i) for i in range(2)]
            sn = [K.sb([128, 320], F32, "pa_sin%d" % i) for i in range(2)]
            ublk = [K.sb([128, 4, 512], BF16, "pa_u%d" % i) for i in range(2)]
            vblk = [K.sb([128, 4, 128], BF16, "pa_v%d" % i) for i in range(2)]
            qTb = [K.sb([128, 5, 512], BF16, "pa_qT%d" % i) for i in range(2)]
            tiles = [(sname, L, b0, ti) for sname, L in seqs for b0 in range(0, L, 512) for ti in range(4)]
            blk_of = {}
            for idx, (sname, L, b0, ti) in enumerate(tiles):
                blk_of[idx] = idx // 4

            def stage_a(idx):
                sname, L, b0, ti = tiles[idx]
                bi = blk_of[idx] % 2
                i2 = idx % 2
                t0 = b0 + ti * 128
                x_t = xa[i2]
                sqt, ss, qk, qkr, t1, t2 = sqt_l[i2], ss_l[i2], qk_l[i2], qkr_l[i2], t1_l[i2], t2_l[i2]
                pu, pq, kvt, ko = pp_u[i2], pp_q[i2], kv_h[i2], i2 * 256
                pouts = [(pu, pu[:, 0:512]), (pq, pq[:, 0:512]), (kvt, kvt[:, ko:ko + 256])]
                K.dma("sp", x_t[:], xin[sname][t0:t0 + 128, :], writes=[x_t])
                K.dma("sp", cs[i2][:], rope_cos[t0:t0 + 128, :], writes=[cs[i2]])
                K.dma("sp", sn[i2][:], rope_sin[t0:t0 + 128, :], writes=[sn[i2]])
                hT_t = hTa[i2]
                norm_to_T(x_t, 128, gam, hT_t, hT_t[:, :, :])
                yield
                for c3, (n0, nn) in enumerate(((0, 512), (512, 512), (1024, 256))):
                    for k in range(8):
                        K.op("pe", lambda e: e.matmul(out=pouts[c3][1], lhsT=hT_t[:, k, :],
                                                      rhs=win[:, k, n0:n0 + nn], start=(k == 0), stop=(k == 7)),
                             reads=[hT_t, win], writes=[pouts[c3][0]], inc=(k == 7))
                K.op("act", lambda e: e.copy(out=ublk[bi][:, ti, :], in_=pu[:, :]),
                     reads=[pu], writes=[ublk[bi]])
                K.op("act", lambda e: e.copy(out=vblk[bi][:, ti, :], in_=kvt[:, ko + 128:ko + 256]),
                     reads=[kvt], writes=[vblk[bi]])
                K.op("act", lambda e: e.activation(out=sqt[:, 0:512], in_=pq[:, :], func=AF.Square),
                     reads=[pq], writes=[sqt])
                K.op("act", lambda e: e.activation(out=sqt[:, 512:640], in_=kvt[:, ko:ko + 128], func=AF.Square),
                     reads=[kvt], writes=[sqt])
                yield
                K.op("dve", lambda e: e.tensor_reduce(out=ss[:, 0:10],
                                                      in_=sqt[:, :].rearrange("p (h d) -> p h d", d=64),
                                                      axis=AX.X, op=ALU.add), reads=[sqt], writes=[ss])
                K.op("pool", lambda e: e.tensor_scalar(out=ss[:, 10:20], in0=ss[:, 0:10], scalar1=1.0 / 64,
                                                       scalar2=EPS, op0=ALU.mult, op1=ALU.add),
                     reads=[ss], writes=[ss])
                K.op("pool", lambda e: e.tensor_tensor(out=ss[:, 30:40], in0=ss[:, 10:20],
                                                       in1=neghalf[:, 0:1].to_broadcast([128, 10]), op=ALU.pow),
                     reads=[ss, neghalf], writes=[ss])
                qk3 = qk[:, :].rearrange("p (h d) -> p h d", d=64)
                K.op("dve", lambda e: e.tensor_tensor(
                    out=qk3[:, 0:8, :], in0=pq[:, :].rearrange("p (h d) -> p h d", d=64),
                    in1=ss[:, 30:38].unsqueeze(2).to_broadcast([128, 8, 64]), op=ALU.mult),
                    reads=[pq, ss], writes=[qk])
                K.op("dve", lambda e: e.tensor_tensor(
                    out=qk3[:, 8:10, :], in0=kvt[:, ko:ko + 128].rearrange("p (h d) -> p h d", d=64),
                    in1=ss[:, 38:40].unsqueeze(2).to_broadcast([128, 2, 64]), op=ALU.mult),
                    reads=[kvt, ss], writes=[qk])
                K.op("dve", lambda e: e.tensor_tensor(out=qk3, in0=qk3, in1=gain[:, :, :], op=ALU.mult),
                     reads=[qk, gain], writes=[qk])
                qv = qk[:, :].rearrange("p (g f i) -> p g f i", f=2, i=16)
                ov = qkr[:, :].rearrange("p (g f i) -> p g f i", f=2, i=16)
                cv_ = cs[i2][:, :].rearrange("p (g i) -> p g i", i=16)
                sv_ = sn[i2][:, :].rearrange("p (g i) -> p g i", i=16)
                t1v = t1[:, :].rearrange("p (g i) -> p g i", i=16)
                t2v = t2[:, :].rearrange("p (g i) -> p g i", i=16)
                K.op("dve", lambda e: e.tensor_tensor(out=t1v, in0=qv[:, :, 0, :], in1=cv_, op=ALU.mult),
                     reads=[qk, cs[i2]], writes=[t1])
                K.op("pool", lambda e: e.tensor_tensor(out=t2v, in0=qv[:, :, 1, :], in1=sv_, op=ALU.mult),
                     reads=[qk, sn[i2]], writes=[t2])
                K.op("dve", lambda e: e.tensor_tensor(out=ov[:, :, 0, :], in0=t1v, in1=t2v, op=ALU.subtract),
                     reads=[t1, t2], writes=[qkr])
                K.op("dve", lambda e: e.tensor_tensor(out=t1v, in0=qv[:, :, 0, :], in1=sv_, op=ALU.mult),
                     reads=[qk, sn[i2]], writes=[t1])
                K.op("pool", lambda e: e.tensor_tensor(out=t2v, in0=qv[:, :, 1, :], in1=cv_, op=ALU.mult),
                     reads=[qk, cs[i2]], writes=[t2])
                K.op("dve", lambda e: e.tensor_tensor(out=ov[:, :, 1, :], in0=t1v, in1=t2v, op=ALU.add),
                     reads=[t1, t2], writes=[qkr])

            def stage_b(idx):
                sname, L, b0, ti = tiles[idx]
                bi = blk_of[idx] % 2
                i2 = idx % 2
                qkr = qkr_l[i2]
                qr3 = qkr[:, :].rearrange("p (h d) -> p h d", d=64)
                for pi in range(5):
                    src = qkr[:, pi * 128:(pi + 1) * 128]
                    K.op("pe", lambda e: e.transpose(out=tq[:, pi, :], in_=src, identity=ident_b[:, :]),
                         reads=[qkr, ident_b], writes=[tq], inc=(pi == 4))
                K.op("act", lambda e: e.copy(out=qTb[bi][:, :, ti * 128:(ti + 1) * 128], in_=tq[:, :, :]),
                     reads=[tq], writes=[qTb[bi]])
                if ti == 3:
                    K.dma("pool", u_d[sname][b0:b0 + 512, :].rearrange("(t p) c -> p t c", p=128), ublk[bi][:, :, :],
                          reads=[ublk[bi]])
                    K.dma("pool", v_d[sname][b0:b0 + 512, :].rearrange("(t p) c -> p t c", p=128), vblk[bi][:, :, :],
                          reads=[vblk[bi]])
                    K.dma("pool", qT_d[sname][:, :, b0:b0 + 512].rearrange("i p t -> p i t"), qTb[bi][:, 0:4, :],
                          reads=[qTb[bi]])
                    K.dma("pool", kT_d[sname][:, b0:b0 + 512], qTb[bi][:, 4, :], reads=[qTb[bi]])


            def tile_gen(par):
                for idx in range(par, len(tiles), 2):
                    yield from stage_a(idx)
                    yield
                    stage_b(idx)
                    yield

            run_interleaved([tile_gen(0), tile_gen(1)], lead=[1, 0])

        def attn_phase():
            gq = load_rep(q_norm[0, :], 64, "gq_rep")
            gk = load_rep(k_norm[0, :], 64, "gk_rep")
            mm_ = K.sb([128, 4], F32, "negm")
            K.op("dve", lambda e: e.tensor_reduce(out=mm_[:, 0:1], in_=gq[:, :], axis=AX.X, op=ALU.max, apply_absolute_value=True),
                 reads=[gq], writes=[mm_])
            K.op("dve", lambda e: e.tensor_reduce(out=mm_[:, 1:2], in_=gk[:, :], axis=AX.X, op=ALU.max, apply_absolute_value=True),
                 reads=[gk], writes=[mm_])
            K.op("dve", lambda e: e.tensor_tensor(out=mm_[:, 2:3], in0=mm_[:, 0:1], in1=mm_[:, 1:2], op=ALU.mult),
                 reads=[mm_], writes=[mm_])
            K.op("dve", lambda e: e.tensor_scalar(out=mm_[:, 3:4], in0=mm_[:, 2:3], scalar1=-8.0, scalar2=None,
                                                  op0=ALU.mult), reads=[mm_], writes=[mm_])
            sps = [K.ps([128, 2, 512], F32, "at_s%d" % i) for i in range(3)]
            ops2 = [[K.ps([128, 512], F32, "at_o%d_%d" % (i, j)) for j in range(2)] for i in range(1)] * 2
            dns = [K.sb([65, 512], F32, "at_dn%d" % i) for i in range(2)]
            pT = [K.sb([128, 2, 512], BF16, "at_p%d" % i) for i in range(4)]
            qTt = [K.sb([128, 512], BF16, "at_q%d" % i) for i in range(2)]
            ao = [K.sb([64, 512], BF16, "at_ao%d" % i) for i in range(2)]
            ns = 0
            nq = 0
            no = 0
            for sname, L in seqs:
                NT = L // 128
                kT = K.sb([128, L], BF16, "at_kT_" + sname)
                K.dma("sp", kT[:, :], kT_d[sname][:, :], writes=[kT])
                va = K.sb([128, NT, 2, 65], BF16, "at_va_" + sname)
                K.op("pool", lambda e: e.memset(va[:, :, :, 64:65], 1.0), writes=[va])
                for hv in range(2):
                    K.dma("sp", va[:, :, hv, 0:64],
                          v_d[sname][:, hv * 64:(hv + 1) * 64].rearrange("(t p) d -> p t d", p=128), writes=[va])
                for q0 in range(0, L, 512):
                    for pi in range(4):
                        qt = qTt[nq % 2]
                        ops = ops2[nq % 2]
                        nq += 1
                        K.dma("sp", qt[:, :], qT_d[sname][pi, :, q0:q0 + 512], writes=[qt])
                        pendq = []
                        for kt in range(NT + 2):
                            if kt < NT:
                                s_ps = sps[ns % 3]
                                p_sb = pT[ns % 4]
                                ns += 1
                                for hh in range(2):
                                    r0 = hh * 64
                                    K.op("pe", lambda e: e.matmul(out=s_ps[:, hh, :], lhsT=kT[r0:r0 + 64, kt * 128:(kt + 1) * 128],
                                                                  rhs=qt[r0:r0 + 64, :], start=True, stop=True),
                                         reads=[kT, qt], writes=[s_ps], inc=(hh == 1))
                                K.op("act", lambda e: e.activation(out=p_sb[:, :, :], in_=s_ps[:, :, :], func=AF.Exp,
                                                                   bias=mm_[:, 3:4], scale=0.125),
                                     reads=[s_ps, mm_], writes=[p_sb])
                            if kt < NT:
                                pendq.append((kt, p_sb))
                            if len(pendq) > 2 or (kt >= NT and pendq):
                                pk, pp_sb = pendq.pop(0)
                                for hh in range(2):
                                    K.op("pe", lambda e: e.matmul(out=ops[hh][0:65, :], lhsT=va[:, pk, hh, :],
                                                                  rhs=pp_sb[:, hh, :], start=(pk == 0), stop=(pk == NT - 1)),
                                         reads=[va, pp_sb], writes=[ops[hh]], inc=(pk == NT - 1))
                        for hh in range(2):
                            h = pi + 4 * hh
                            o_ps = ops[hh]
                            a_o = ao[no % 2]
                            dn = dns[no % 2]
                            no += 1
                            K.op("act", lambda e: e.copy(out=a_o[:, :], in_=o_ps[0:64, :]), reads=[o_ps], writes=[a_o])
                            K.op("act", lambda e: e.copy(out=dn[64:65, :], in_=o_ps[64:65, :]), reads=[o_ps], writes=[dn])
                            K.dma("sp", aT_d[sname][h * 64:(h + 1) * 64, q0:q0 + 512], a_o[:, :], reads=[a_o])
                            K.dma("sp", den2_d[sname][h:h + 1, q0:q0 + 512], dn[64:65, :], reads=[dn])

        s5p = {nm: din(nm, shp) for nm, shp in (
            ("s5_lambda_re", [1, 2, 32, 64]), ("s5_lambda_im", [1, 2, 32, 64]), ("s5_log_dt", [1, 2, 32]),
            ("s5_b_re", [1, 2, 32, 64, 16]), ("s5_b_im", [1, 2, 32, 64, 16]),
            ("s5_c_re", [1, 2, 32, 16, 64]), ("s5_c_im", [1, 2, 32, 16, 64]),
            ("s5_d", [1, 512]), ("s5_w_glu", [1, 512, 512]), ("s5_b_glu", [1, 512]))}
        maskf_in = din("maskf", [128, 128])
        maskb_in = din("maskb", [128, 128])
        Hf_d = {s: dscr("Hf_" + s, [64, 32, 2, L // 8 + 1], BF16) for s, L in seqs}
        Hb_d = {s: dscr("Hb_" + s, [64, 32, 2, L // 8 + 1], BF16) for s, L in seqs}
        S_all_d = {s: dscr("Sall_" + s, [L // 1024, 128, 32 * 2 * 128], F32) for s, L in seqs}
        PI = float(np.pi)

        def s5_phase(attn_fn):
            s5es = ExitStack()
            outer = K.es
            K.es = s5es
            Bc = K.sb([128, 2, 32, 128], BF16, "s5_Bc")
            CcRb = K.sb([128, 32, 128], BF16, "s5_CcRb")
            CcIb = K.sb([128, 32, 128], BF16, "s5_CcIb")
            Wb = K.sb([128, 32, 128], BF16, "s5_W")
            A8 = K.sb([128, 3, 32], F32, "s5_A8")
            wglu = K.sb([128, 4, 512], BF16, "s5_wglu")
            bglu = K.sb([128, 512], F32, "s5_bglu")
            K.es = outer

            def tt(eng, out, in0, in1, op, reads, writes):
                K.op(eng, lambda e: e.tensor_tensor(out=out, in0=in0, in1=in1, op=op), reads=reads, writes=writes)

            def setup():
                K.dma("pool", wglu[:, :, :], s5p["s5_w_glu"][0].rearrange("(k p) n -> p k n", p=128), writes=[wglu])
                K.dma("sp", bglu[:, :], s5p["s5_b_glu"][0, :].partition_broadcast(128), writes=[bglu])
                lr = K.sb([128, 32], F32, "lr")
                li = K.sb([128, 32], F32, "li")
                ldt = K.sb([128, 32], F32, "ldt")
                Br = K.sb([128, 32, 16], F32, "Br")
                Bi = K.sb([128, 32, 16], F32, "Bi")
                Cr = K.sb([128, 32, 16], F32, "Cr")
                Ci = K.sb([128, 32, 16], F32, "Ci")
                for d in range(2):
                    ps_ = slice(d * 64, (d + 1) * 64)
                    K.dma("sp", lr[ps_, :], s5p["s5_lambda_re"][0, d].rearrange("g p -> p g"), writes=[lr],
                          allow_slow_non_contiguous=True)
                    K.dma("sp", li[ps_, :], s5p["s5_lambda_im"][0, d].rearrange("g p -> p g"), writes=[li],
                          allow_slow_non_contiguous=True)
                    K.dma("sp", ldt[ps_, :], s5p["s5_log_dt"][0, d, :].partition_broadcast(64), writes=[ldt])
                    K.dma("sp", Br[ps_, :, :], s5p["s5_b_re"][0, d].rearrange("g p c -> p g c"), writes=[Br])
                    K.dma("sp", Bi[ps_, :, :], s5p["s5_b_im"][0, d].rearrange("g p c -> p g c"), writes=[Bi])
                    for g4 in range(4):
                        gs = slice(g4 * 8, (g4 + 1) * 8)
                        K.dma("sp", Cr[ps_, gs, :], s5p["s5_c_re"][0, d, gs].rearrange("g c p -> p g c"), writes=[Cr],
                              allow_slow_non_contiguous=True)
                        K.dma("sp", Ci[ps_, gs, :], s5p["s5_c_im"][0, d, gs].rearrange("g c p -> p g c"), writes=[Ci],
                              allow_slow_non_contiguous=True)
                maskf = K.sb([128, 128], F32, "maskf")
                maskb = K.sb([128, 128], F32, "maskb")
                K.dma("sp", maskf[:, :], maskf_in, writes=[maskf])
                K.dma("sp", maskb[:, :], maskb_in, writes=[maskb])
                dcol = K.sb([128, 32], F32, "dcol")
                for j in range(8):
                    K.dma("sp", dcol[j * 16:(j + 1) * 16, :], s5p["s5_d"][0, :].rearrange("(g c) -> c g", c=16),
                          writes=[dcol], allow_slow_non_contiguous=True)
                cst = K.sb([128, 2], F32, "cst")
                if S5CUT == 1:
                    return
                K.op("pool", lambda e: e.memset(cst[:, 0:1], -PI), writes=[cst])
                dt = K.sb([128, 32], F32, "dt")
                K.op("act", lambda e: e.activation(out=dt[:, :], in_=ldt[:, :], func=AF.Exp), reads=[ldt], writes=[dt])
                lrdt = K.sb([128, 32], F32, "lrdt")
                lidt = K.sb([128, 32], F32, "lidt")
                tt("dve", lrdt[:, :], lr[:, :], dt[:, :], ALU.mult, [lr, dt], [lrdt])
                tt("dve", lidt[:, :], li[:, :], dt[:, :], ALU.mult, [li, dt], [lidt])
                ang = K.sb([128, 9, 32], F32, "ang")
                lmag = K.sb([128, 10, 32], F32, "lmag")
                for e_ in range(9):
                    K.op("dve", lambda e: e.tensor_scalar(out=ang[:, e_, :], in0=lidt[:, :], scalar1=float(e_), scalar2=None,
                                                          op0=ALU.mult), reads=[lidt], writes=[ang])
                    K.op("dve", lambda e: e.tensor_scalar(out=lmag[:, e_, :], in0=lrdt[:, :], scalar1=float(e_), scalar2=None,
                                                          op0=ALU.mult), reads=[lrdt], writes=[lmag])
                K.op("dve", lambda e: e.tensor_scalar(out=lmag[:, 9, :], in0=lrdt[:, :], scalar1=-16.0, scalar2=None,
                                                      op0=ALU.mult), reads=[lrdt], writes=[lmag])
                mag = K.sb([128, 10, 32], F32, "mag")
                K.op("act", lambda e: e.activation(out=mag[:, :, :], in_=lmag[:, :, :], func=AF.Exp), reads=[lmag], writes=[mag])
                sn = K.sb([128, 9, 32], F32, "sn")
                cs = K.sb([128, 9, 32], F32, "cs")
                kf = K.sb([128, 9, 32], F32, "kf")
                ki = K.sb([128, 9, 32], mybir.dt.int32, "ki")
                angp = K.sb([128, 9, 32], F32, "angp")
                for dst, shift in ((sn, 0.0), (cs, 0.5 * PI)):
                    K.op("dve", lambda e: e.tensor_scalar(out=angp[:, :, :], in0=ang[:, :, :], scalar1=shift, scalar2=None,
                                                          op0=ALU.add), reads=[ang], writes=[angp])
                    K.op("dve", lambda e: e.tensor_scalar(out=kf[:, :, :], in0=angp[:, :, :], scalar1=1.0 / (2 * PI),
                                                          scalar2=None, op0=ALU.mult), reads=[angp], writes=[kf])
                    K.op("dve", lambda e: e.tensor_copy(out=ki[:, :, :], in_=kf[:, :, :]), reads=[kf], writes=[ki])
                    K.op("dve", lambda e: e.tensor_copy(out=kf[:, :, :], in_=ki[:, :, :]), reads=[ki], writes=[kf])
                    K.op("dve", lambda e: e.scalar_tensor_tensor(out=angp[:, :, :], in0=kf[:, :, :], scalar=-2 * PI,
                                                                 in1=angp[:, :, :], op0=ALU.mult, op1=ALU.add),
                         reads=[kf, angp], writes=[angp])
                    K.op("dve", lambda e: e.tensor_scalar(out=angp[:, :, :], in0=angp[:, :, :], scalar1=-3.1415925,
                                                          scalar2=3.1415925, op0=ALU.max, op1=ALU.min),
                         reads=[angp], writes=[angp])
                    K.op("act", lambda e: e.activation(out=dst[:, :, :], in_=angp[:, :, :], func=AF.Sin),
                         reads=[angp], writes=[dst])
                Er = K.sb([128, 9, 32], F32, "Er")
                Ei = K.sb([128, 9, 32], F32, "Ei")
                tt("dve", Er[:, :, :], mag[:, 0:9, :], cs[:, :, :], ALU.mult, [mag, cs], [Er])
                tt("dve", Ei[:, :, :], mag[:, 0:9, :], sn[:, :, :], ALU.mult, [mag, sn], [Ei])
                if S5CUT == 2:
                    return
                K.op("dve", lambda e: e.tensor_copy(out=A8[:, 0, :], in_=Er[:, 8, :]), reads=[Er], writes=[A8])
                K.op("dve", lambda e: e.tensor_copy(out=A8[:, 1, :], in_=Ei[:, 8, :]), reads=[Ei], writes=[A8])
                K.op("dve", lambda e: e.tensor_scalar(out=A8[:, 2, :], in0=Ei[:, 8, :], scalar1=-1.0, scalar2=None,
                                                      op0=ALU.mult), reads=[Ei], writes=[A8])
                w_ = K.sb([128, 8, 32], F32, "zwork")
                tt("dve", w_[:, 0, :], lr[:, :], lr[:, :], ALU.mult, [lr], [w_])
                tt("dve", w_[:, 1, :], li[:, :], li[:, :], ALU.mult, [li], [w_])
                tt("dve", w_[:, 0, :], w_[:, 0, :], w_[:, 1, :], ALU.add, [w_], [w_])
                K.op("dve", lambda e: e.reciprocal(out=w_[:, 1, :], in_=w_[:, 0, :]), reads=[w_], writes=[w_])
                K.op("dve", lambda e: e.tensor_scalar(out=w_[:, 2, :], in0=Er[:, 1, :], scalar1=-1.0, scalar2=None,
                                                      op0=ALU.add), reads=[Er], writes=[w_])
                tt("dve", w_[:, 3, :], w_[:, 2, :], lr[:, :], ALU.mult, [w_, lr], [w_])
                tt("dve", w_[:, 4, :], Ei[:, 1, :], li[:, :], ALU.mult, [Ei, li], [w_])
                tt("dve", w_[:, 3, :], w_[:, 3, :], w_[:, 4, :], ALU.add, [w_], [w_])
                tt("dve", w_[:, 5, :], Ei[:, 1, :], lr[:, :], ALU.mult, [Ei, lr], [w_])
                tt("dve", w_[:, 6, :], w_[:, 2, :], li[:, :], ALU.mult, [w_, li], [w_])
                tt("dve", w_[:, 5, :], w_[:, 5, :], w_[:, 6, :], ALU.subtract, [w_], [w_])
                zr = K.sb([128, 32], F32, "zr")
                zi = K.sb([128, 32], F32, "zi")
                tt("dve", zr[:, :], w_[:, 3, :], w_[:, 1, :], ALU.mult, [w_], [zr])
                tt("dve", zi[:, :], w_[:, 5, :], w_[:, 1, :], ALU.mult, [w_], [zi])
                Gr = K.sb([128, 32], F32, "Gr")
                Gi = K.sb([128, 32], F32, "Gi")
                tt("dve", Gr[:, :], Er[:, 8, :], mag[:, 9, :], ALU.mult, [Er, mag], [Gr])
                tt("dve", Gi[:, :], A8[:, 2, :], mag[:, 9, :], ALU.mult, [A8, mag], [Gi])
                t16a = K.sb([128, 32, 16], F32, "t16a")
                t16b = K.sb([128, 32, 16], F32, "t16b")
                Bbr = K.sb([128, 32, 16], F32, "Bbr")
                Bbi = K.sb([128, 32, 16], F32, "Bbi")
                zrb = zr[:, :].unsqueeze(2).to_broadcast([128, 32, 16])
                zib = zi[:, :].unsqueeze(2).to_broadcast([128, 32, 16])
                tt("dve", t16a[:, :, :], Br[:, :, :], zrb, ALU.mult, [Br, zr], [t16a])
                tt("dve", t16b[:, :, :], Bi[:, :, :], zib, ALU.mult, [Bi, zi], [t16b])
                tt("dve", Bbr[:, :, :], t16a[:, :, :], t16b[:, :, :], ALU.subtract, [t16a, t16b], [Bbr])
                tt("dve", t16a[:, :, :], Bi[:, :, :], zrb, ALU.mult, [Bi, zr], [t16a])
                tt("dve", t16b[:, :, :], Br[:, :, :], zib, ALU.mult, [Br, zi], [t16b])
                tt("dve", Bbi[:, :, :], t16a[:, :, :], t16b[:, :, :], ALU.add, [t16a, t16b], [Bbi])
                EBr = K.sb([128, 8, 32], F32, "EBr")
                EBi = K.sb([128, 8, 32], F32, "EBi")
                ECr = K.sb([128, 8, 32], F32, "ECr")
                ECi = K.sb([128, 8, 32], F32, "ECi")
                for (EB, EC, E) in ((EBr, ECr, Er), (EBi, ECi, Ei)):
                    for j in range(8):
                        K.op("dve", lambda e: e.tensor_copy(out=EB[0:64, j, :], in_=E[0:64, 7 - j, :]), reads=[E], writes=[EB])
                        K.op("dve", lambda e: e.tensor_copy(out=EC[64:128, j, :], in_=E[64:128, 8 - j, :]), reads=[E], writes=[EC])
                    K.op("dve", lambda e: e.tensor_copy(out=EB[64:128, :, :], in_=E[64:128, 0:8, :]), reads=[E], writes=[EB])
                    K.op("dve", lambda e: e.tensor_copy(out=EC[0:64, :, :], in_=E[0:64, 1:9, :]), reads=[E], writes=[EC])
                big = [K.sb([128, 32, 8, 16], F32, "s5big%d" % i) for i in range(4)]
                BcR, BcI, T1, T2 = big
                CcR, CcI = BcR, BcI
                BcRb_t = K.sb([128, 32, 128], BF16, "BcRb")
                BcIb_t = K.sb([128, 32, 128], BF16, "BcIb")
                YRb_t = K.sb([128, 32, 128], BF16, "YRb")
                YIb_t = K.sb([128, 32, 128], BF16, "YIb")

                def outer_prod(dst, X, Et, reads):
                    tt("dve", dst[:, :, :, :], X[:, :, :].unsqueeze(2).to_broadcast([128, 32, 8, 16]),
                       Et[:, :, :].rearrange("p j g -> p g j").unsqueeze(3).to_broadcast([128, 32, 8, 16]), ALU.mult,
                       reads, [dst])

                def full(t):
                    return t[:, :, :, :]

                outer_prod(T1, Bbr, EBr, [Bbr, EBr])
                outer_prod(T2, Bbi, EBi, [Bbi, EBi])
                tt("dve", full(BcR), full(T1), full(T2), ALU.subtract, [T1, T2], [BcR])
                outer_prod(T1, Bbr, EBi, [Bbr, EBi])
                outer_prod(T2, Bbi, EBr, [Bbi, EBr])
                tt("dve", full(BcI), full(T1), full(T2), ALU.add, [T1, T2], [BcI])
                tps = [K.ps([128, 4, 128], F32, "s5tps%d" % i) for i in range(2)]
                nq = 0
                for d in range(2):
                    ps_ = slice(d * 64, (d + 1) * 64)
                    for g0 in range(0, 32, 4):
                        tp_ = tps[nq % 2]
                        nq += 1
                        for gi in range(4):
                            g = g0 + gi
                            for ri, Bx in enumerate((BcR, BcI)):
                                K.op("pe", lambda e: e.transpose(out=tp_[:, gi, ri * 64:(ri + 1) * 64],
                                                                 in_=Bx[ps_, g, :, :].rearrange("p j c -> p (j c)"),
                                                                 identity=ident_f[ps_, ps_]),
                                     reads=[Bx, ident_f], writes=[tp_], inc=(gi == 3 and ri == 1))
                        K.op("act", lambda e: e.copy(out=Bc[:, d, g0:g0 + 4, :], in_=tp_[:, :, :]), reads=[tp_], writes=[Bc])
                K.op("act", lambda e: e.copy(out=BcRb_t[:, :, :], in_=BcR[:, :, :, :].rearrange("p g t c -> p g (t c)")),
                     reads=[BcR], writes=[BcRb_t])
                K.op("act", lambda e: e.copy(out=BcIb_t[:, :, :], in_=BcI[:, :, :, :].rearrange("p g t c -> p g (t c)")),
                     reads=[BcI], writes=[BcIb_t])
                outer_prod(T1, Cr, ECr, [Cr, ECr])
                outer_prod(T2, Ci, ECi, [Ci, ECi])
                tt("dve", full(CcR), full(T1), full(T2), ALU.subtract, [T1, T2], [CcR])
                outer_prod(T1, Cr, ECi, [Cr, ECi])
                outer_prod(T2, Ci, ECr, [Ci, ECr])
                tt("dve", full(T1), full(T1), full(T2), ALU.add, [T1, T2], [T1])
                K.op("dve", lambda e: e.tensor_scalar(out=full(CcI), in0=full(T1), scalar1=-1.0, scalar2=None, op0=ALU.mult),
                     reads=[T1], writes=[CcI])
                K.op("act", lambda e: e.copy(out=CcRb[:, :, :], in_=CcR[:, :, :, :].rearrange("p g t c -> p g (t c)")),
                     reads=[CcR], writes=[CcRb])
                K.op("act", lambda e: e.copy(out=CcIb[:, :, :], in_=CcI[:, :, :, :].rearrange("p g t c -> p g (t c)")),
                     reads=[CcI], writes=[CcIb])
                if S5CUT == 3:
                    return
                Grb = Gr[:, :].unsqueeze(2).to_broadcast([128, 32, 128])
                if S5CUT == 4:
                    return
                Gib = Gi[:, :].unsqueeze(2).to_broadcast([128, 32, 128])

                def v3(t):
                    return t[:, :, :, :].rearrange("p g t c -> p g (t c)")

                tt("dve", v3(T1), v3(CcR), Grb, ALU.mult, [CcR, Gr], [T1])
                tt("dve", v3(T2), v3(CcI), Gib, ALU.mult, [CcI, Gi], [T2])
                tt("dve", v3(T1), v3(T1), v3(T2), ALU.add, [T1, T2], [T1])
                tt("dve", v3(T2), v3(CcI), Grb, ALU.mult, [CcI, Gr], [T2])
                tt("dve", v3(CcR), v3(CcR), Gib, ALU.mult, [CcR, Gi], [CcR])
                tt("dve", v3(T2), v3(T2), v3(CcR), ALU.subtract, [T2, CcR], [T2])
                YR, YI = T1, T2
                if S5CUT == 5:
                    return
                wt = [K.sb([128, 128], F32, "wt%d" % i) for i in range(2)]
                K.op("act", lambda e: e.copy(out=YRb_t[:, :, :], in_=v3(YR)), reads=[YR], writes=[YRb_t])
                K.op("act", lambda e: e.copy(out=YIb_t[:, :, :], in_=v3(YI)), reads=[YI], writes=[YIb_t])
                BcRb, BcIb, YRb, YIb = BcRb_t, BcIb_t, YRb_t, YIb_t
                for g in range(32):
                    for d in range(2):
                        ps_ = slice(d * 64, (d + 1) * 64)
                        tpd = tps[d]
                        K.op("pe", lambda e: e.matmul(out=tpd[:, g % 4, :], lhsT=BcRb[ps_, g, :], rhs=YRb[ps_, g, :],
                                                      start=True, stop=False),
                             reads=[BcRb_t, YRb_t], writes=[tpd], inc=False)
                        K.op("pe", lambda e: e.matmul(out=tpd[:, g % 4, :], lhsT=BcIb[ps_, g, :], rhs=YIb[ps_, g, :],
                                                      start=False, stop=True),
                             reads=[BcIb_t, YIb_t], writes=[tpd], inc=True)
                    if S5CUT == 7:
                        continue
                    tt("dve", wt[0][:, :], tps[0][:, g % 4, :], maskf[:, :], ALU.mult, [tps[0], maskf], [wt[0]])
                    tt("dve", wt[1][:, :], tps[1][:, g % 4, :], maskb[:, :], ALU.mult, [tps[1], maskb], [wt[1]])
                    tt("dve", wt[0][:, :], wt[0][:, :], wt[1][:, :], ALU.add, [wt[0], wt[1]], [wt[0]])
                    K.op("dve", lambda e: e.scalar_tensor_tensor(out=Wb[:, g, :], in0=ident_f[:, :], scalar=dcol[:, g:g + 1],
                                                                 in1=wt[0][:, :], op0=ALU.mult, op1=ALU.add),
                         reads=[ident_f, dcol, wt[0]], writes=[Wb])

            K.run_phase(setup)
            if "s5stop0" in phases:
                s5es.close()
                return

            def u8_factory():
                ucm = [K.sb([128, 8, 512], BF16, "ucm%d" % i) for i in range(2)]
                ugm = [K.sb([128, 32, 128], BF16, "ugm%d" % i) for i in range(2)]
                u8 = [K.sb([128, 32, 128], BF16, "u8_%d" % i) for i in range(2)]
                ups = [K.ps([128, 4, 128], F32, "u8ps%d" % i) for i in range(2)]
                cnt = {"ps": 0}

                def make_u8(sname, b, slot):
                    uc = ucm[slot]
                    K.dma("sp", uc[:, :, :], u_d[sname][b * 1024:(b + 1) * 1024, :].rearrange("(c j) ch -> c j ch", j=8),
                          writes=[uc])
                    ug = ugm[slot]
                    K.op("act", lambda e: e.copy(out=ug[:, :, :].rearrange("p g (j c) -> p g j c", c=16),
                                                 in_=uc[:, :, :].rearrange("p j (g c) -> p g j c", c=16)),
                         reads=[uc], writes=[ug])
                    u8t = u8[slot]
                    for g0 in range(0, 32, 4):
                        pt = ups[cnt["ps"] % 2]
                        cnt["ps"] += 1
                        for gi in range(4):
                            K.op("pe", lambda e: e.matmul(out=pt[:, gi, :], lhsT=ug[:, g0 + gi, :], rhs=ident_b[:, :],
                                                          start=True, stop=True), reads=[ug, ident_b], writes=[pt], inc=(gi == 3))
                        K.op("dve", lambda e: e.tensor_copy(out=u8t[:, g0:g0 + 4, :], in_=pt[:, :, :]), reads=[pt], writes=[u8t])
                    return u8t

                return make_u8

            def s5_pre():
                make_u8 = u8_factory()
                sps = [K.ps([128, 2, 2, 128], F32, "s5sps%d" % i) for i in range(2)]
                Sst = [K.sb([128, 32, 2, 128], F32, "s5Sst%d" % i) for i in range(2)]
                nb_ = 0
                nsp = 0
                for sname, L in seqs:
                    for b in range(L // 1024):
                        u8t = make_u8(sname, b, nb_ % 2)
                        st_ = Sst[nb_ % 2]
                        nb_ += 1
                        for g0 in range(0, 32, 2):
                            pt = sps[nsp % 2]
                            nsp += 1
                            for gi in range(2):
                                g = g0 + gi
                                for d in range(2):
                                    for ri in range(2):
                                        K.op("pe", lambda e: e.matmul(out=pt[d * 64:(d + 1) * 64, gi, ri, :],
                                                                      lhsT=Bc[:, d, g, ri * 64:(ri + 1) * 64],
                                                                      rhs=u8t[:, g, :], start=True, stop=True),
                                             reads=[Bc, u8t], writes=[pt], inc=(gi == 1 and d == 1 and ri == 1))
                            K.op("act", lambda e: e.copy(out=st_[:, g0:g0 + 2, :, :], in_=pt[:, :, :, :]),
                                 reads=[pt], writes=[st_])
                        K.dma("pool", S_all_d[sname][b], st_[:, :, :, :].rearrange("p g r c -> p (g r c)"), reads=[st_])

            def recurrence():
                S_buf = [K.sb([128, 32, 2, 128], F32, "s5S%d" % i) for i in range(2)]
                S_w = [[Tl(S_buf[i].t, "s5S%d_%d" % (i, d)) for d in range(2)] for i in range(2)]
                Hb16 = K.sb([128, 32, 2, 128], BF16, "s5Hb16")
                Hb16_h = [Hb16, Tl(Hb16.t, "s5Hb16_b")]
                car = K.sb([128, 32, 2], F32, "s5car")
                car_h = [car, Tl(car.t, "s5car_b")]
                tP = K.sb([128, 32, 2], F32, "s5tP")
                tP_h = [tP, Tl(tP.t, "s5tP_b")]
                tQ = K.sb([128, 32, 2], F32, "s5tQ")
                tQ_h = [tQ, Tl(tQ.t, "s5tQ_b")]
                zer = K.sb([128, 64], BF16, "s5zero")
                K.op("pool", lambda e: e.memset(zer[:, :], 0.0), writes=[zer])
                steps = [(sname, L, i) for sname, L in seqs for i in range(L // 1024)]

                def load(k):
                    sname, L, i = steps[k]
                    NB = L // 1024
                    bt = S_buf[k % 2]
                    K.dma("pool", bt[0:64, :, :, :].rearrange("p g r c -> p (g r c)"), S_all_d[sname][i, 0:64, :],
                          writes=[S_w[k % 2][0]])
                    K.dma("pool", bt[64:128, :, :, :].rearrange("p g r c -> p (g r c)"), S_all_d[sname][NB - 1 - i, 64:128, :],
                          writes=[S_w[k % 2][1]])

                load(0)
                for k, (sname, L, i) in enumerate(steps):
                    NB = L // 1024
                    NCH = L // 8
                    if i == 0:
                        K.dma("pool", Hf_d[sname][:, :, :, 0], zer[0:64, :].rearrange("p (g r) -> p g r", r=2), reads=[zer],
                              allow_slow_non_contiguous=True)
                        K.dma("pool", Hb_d[sname][:, :, :, NCH], zer[0:64, :].rearrange("p (g r) -> p g r", r=2), reads=[zer],
                              allow_slow_non_contiguous=True)
                        K.op("dve", lambda e: e.memset(car[0:64, :, :], 0.0), writes=[car_h[0]])
                        K.op("pool", lambda e: e.memset(car[64:128, :, :], 0.0), writes=[car_h[1]])
                    if k + 1 < len(steps):
                        load(k + 1)
                    S_t = S_buf[k % 2]
                    blk = (i, NB - 1 - i)
                    for d, eng in ((0, "dve"), (1, "pool")):
                        ps_ = slice(d * 64, (d + 1) * 64)
                        Sw = S_w[k % 2][d]
                        a8r = A8[ps_, 0, :].unsqueeze(2).to_broadcast([64, 32, 2])
                        order = range(128) if d == 0 else range(127, -1, -1)
                        first = True
                        for s_ in order:
                            if first:
                                prev = car[ps_, :, :]
                                prd = [car_h[d]]
                            else:
                                prev = S_t[ps_, :, :, s_ - 1 if d == 0 else s_ + 1]
                                prd = [Sw]
                            first = False
                            tt(eng, tP[ps_, :, :], prev, a8r, ALU.mult, prd + [A8], [tP_h[d]])
                            tt(eng, tQ[ps_, :, 0], prev[:, :, 1], A8[ps_, 2, :], ALU.mult, prd + [A8], [tQ_h[d]])
                            tt(eng, tQ[ps_, :, 1], prev[:, :, 0], A8[ps_, 1, :], ALU.mult, prd + [A8], [tQ_h[d]])
                            tt(eng, tP[ps_, :, :], tP[ps_, :, :], tQ[ps_, :, :], ALU.add, [tP_h[d], tQ_h[d]], [tP_h[d]])
                            tt(eng, S_t[ps_, :, :, s_], tP[ps_, :, :], S_t[ps_, :, :, s_], ALU.add, [tP_h[d], Sw], [Sw])
                        last = 127 if d == 0 else 0
                        K.op(eng, lambda e: e.tensor_copy(out=car[ps_, :, :], in_=S_t[ps_, :, :, last]),
                             reads=[Sw], writes=[car_h[d]])
                        K.op(eng, lambda e: e.tensor_copy(out=Hb16[ps_, :, :, :], in_=S_t[ps_, :, :, :]),
                             reads=[Sw], writes=[Hb16_h[d]])
                        c0 = blk[d] * 128
                        if d == 0:
                            K.dma("pool", Hf_d[sname][:, :, :, c0 + 1:c0 + 129], Hb16[0:64, :, :, :], reads=[Hb16_h[0]])
                        else:
                            K.dma("pool", Hb_d[sname][:, :, :, c0:c0 + 128], Hb16[64:128, :, :, :], reads=[Hb16_h[1]])

            def sweep2():
                make_u8 = u8_factory()
                Hl = [K.sb([128, 32, 2, 128], BF16, "s5Hl%d" % i) for i in range(2)]
                yps = [K.ps([128, 4, 128], F32, "s5yps%d" % i) for i in range(2)]
                ycm = K.sb([128, 8, 512], F32, "s5ycm")
                yg = K.sb([128, 8, 512], F32, "s5yg")
                ygb = K.sb([128, 8, 512], BF16, "s5ygb")
                so = [K.sb([128, 8, 512], BF16, "s5so%d" % i) for i in range(2)]
                gtp = tp_get()[0]
                yT = [K.sb([128, 4, 128], BF16, "s5yT%d" % i) for i in range(2)]
                gps = [K.ps([128, 512], F32, "s5gps%d" % i) for i in range(2)]
                gsb = [K.sb([128, 512], F32, "s5gsb%d" % i) for i in range(2)]
                nb2 = 0
                ny = 0
                ng = 0
                for sname, L in seqs:
                    NB = L // 1024
                    for b in range(NB):
                        u8t = make_u8(sname, b, nb2 % 2)
                        hl = Hl[nb2 % 2]
                        so_t = so[nb2 % 2]
                        nb2 += 1
                        c0 = b * 128
                        K.dma("sp", hl[0:64, :, :, :], Hf_d[sname][:, :, :, c0:c0 + 128], writes=[hl])
                        K.dma("sp", hl[64:128, :, :, :], Hb_d[sname][:, :, :, c0 + 1:c0 + 129], writes=[hl])
                        for g0 in range(0, 32, 4):
                            pt = yps[ny % 2]
                            ny += 1
                            for gi in range(4):
                                g = g0 + gi
                                K.op("pe", lambda e: e.matmul(out=pt[:, gi, :], lhsT=u8t[:, g, :], rhs=Wb[:, g, :],
                                                              start=True, stop=False), reads=[u8t, Wb], writes=[pt], inc=False)
                                K.op("pe", lambda e: e.matmul(out=pt[:, gi, :], lhsT=hl[:, g, 0, :], rhs=CcRb[:, g, :],
                                                              start=False, stop=False), reads=[hl, CcRb], writes=[pt], inc=False)
                                K.op("pe", lambda e: e.matmul(out=pt[:, gi, :], lhsT=hl[:, g, 1, :], rhs=CcIb[:, g, :],
                                                              start=False, stop=True), reads=[hl, CcIb], writes=[pt],
                                     inc=(gi == 3))
                            K.op("act", lambda e: e.copy(
                                out=ycm[:, :, g0 * 16:(g0 + 4) * 16].rearrange("p t (g c) -> p t g c", c=16),
                                in_=pt[:, :, :].rearrange("p g (t c) -> p t g c", c=16)), reads=[pt], writes=[ycm])
                        gelu_ops(ycm, yg, ygb)
                        for t_ in range(8):
                            yTt = yT[ng % 2]
                            gp = gps[ng % 2]
                            gs_ = gsb[ng % 2]
                            ng += 1
                            for k in range(4):
                                K.op("pe", lambda e: e.transpose(out=gtp[:, k, :], in_=ygb[:, t_, k * 128:(k + 1) * 128],
                                                                 identity=ident_b[:, :]), reads=[ygb, ident_b], writes=[gtp],
                                     inc=(k == 3))
                            K.op("act", lambda e: e.copy(out=yTt[:, :, :], in_=gtp[:, 0:4, :]), reads=[gtp], writes=[yTt])
                            for k in range(4):
                                K.op("pe", lambda e: e.matmul(out=gp[:, :], lhsT=yTt[:, k, :], rhs=wglu[:, k, :],
                                                              start=(k == 0), stop=(k == 3)), reads=[yTt, wglu], writes=[gp],
                                     inc=(k == 3))
                            tt("dve", gs_[:, :], gp[:, :], bglu[:, :], ALU.add, [gp, bglu], [gs_])
                            K.op("act", lambda e: e.activation(out=gs_[:, :], in_=gs_[:, :], func=AF.Sigmoid),
                                 reads=[gs_], writes=[gs_])
                            tt("pool", so_t[:, t_, :], gs_[:, :], yg[:, t_, :], ALU.mult, [gs_, yg], [so_t])
                        K.dma("pool", s5o_d[sname][b * 1024:(b + 1) * 1024, :].rearrange("(c j) ch -> c j ch", j=8),
                              so_t[:, :, :], reads=[so_t])

            def gelu_ops(ycm, yg, ygb):
                a = ycm[:, :, :]
                tt("dve", yg[:, :, :], a, a, ALU.mult, [ycm], [yg])
                K.op("dve", lambda e: e.tensor_scalar(out=yg[:, :, :], in0=yg[:, :, :], scalar1=0.044715, scalar2=1.0,
                                                      op0=ALU.mult, op1=ALU.add), reads=[yg], writes=[yg])
                tt("dve", yg[:, :, :], yg[:, :, :], a, ALU.mult, [yg, ycm], [yg])
                K.op("act", lambda e: e.activation(out=yg[:, :, :], in_=yg[:, :, :], func=AF.Sigmoid,
                                                   scale=2.0 * float(np.sqrt(2.0 / np.pi))), reads=[yg], writes=[yg])
                tt("dve", yg[:, :, :], yg[:, :, :], a, ALU.mult, [yg, ycm], [yg])
                K.op("act", lambda e: e.copy(out=ygb[:, :, :], in_=yg[:, :, :]), reads=[yg], writes=[ygb])

            K.run_phase(s5_pre)

            def attn_and_rec():
                attn_fn()
                recurrence()

            K.run_phase(attn_and_rec)
            K.run_phase(sweep2)
            s5es.close()

        nos5 = "nos5" in phases
        k0 = 4 if nos5 else 0

        def outproj_phase():
            wo = K.sb([128, 8, D], BF16, "wo")
            for k in range(8):
                K.dma("pool", wo[:, k, :], w_out[0, k * 128:(k + 1) * 128, :], writes=[wo])
            xa = [K.sb([128, D], F32, "op_x%d" % i) for i in range(4)]
            s5t = [K.sb([128, 512], BF16, "op_s%d" % i) for i in range(4)]
            catT = [K.sb([128, 8, 128], BF16, "op_c%d" % i) for i in range(4)]
            aTb = [K.sb([128, 4, 512], BF16, "op_ab%d" % i) for i in range(2)]
            dbc = [K.sb([128, 4, 512], F32, "op_db%d" % i) for i in range(2)]
            xo = [K.sb([128, D], F32, "op_xo%d" % i) for i in range(4)]
            tp = K.ps([128, 4, 128], BF16, "op_tp")
            pp = [K.ps([128, 512], F32, "op_ps%d" % i) for i in range(2)]
            otiles = [(sname, t0) for sname, L in seqs for t0 in range(0, L, 128)]
            blk_state = {}

            def stage1(n):
                sname, t0 = otiles[n]
                i2 = n % 4
                K.dma("sp", xa[i2][:], xin[sname][t0:t0 + 128, :], writes=[xa[i2]])
                if not nos5:
                    K.dma("sp", s5t[i2][:], s5o_d[sname][t0:t0 + 128, :], writes=[s5t[i2]])
                cT = catT[i2]
                ab = aTb[(t0 // 512) % 2]
                db = dbc[(t0 // 512) % 2]
                if t0 % 512 == 0:
                    K.dma("sp", ab[:, :, :], aT_d[sname][:, t0:t0 + 512].rearrange("(k p) t -> p k t", p=128),
                          writes=[ab])
                    for h in range(8):
                        K.dma("sp", db[(h % 2) * 64:(h % 2 + 1) * 64, h // 2, :],
                              den2_d[sname][h, t0:t0 + 512].partition_broadcast(64), writes=[db])
                    K.op("dve", lambda e: e.reciprocal(out=db[:, :, :], in_=db[:, :, :]), reads=[db], writes=[db])
                    K.op("dve", lambda e: e.tensor_tensor(out=ab[:, :, :], in0=ab[:, :, :], in1=db[:, :, :], op=ALU.mult),
                         reads=[ab, db], writes=[ab])
                for k in range(4 if not nos5 else 0):
                    K.op("pe", lambda e: e.transpose(out=tp[:, k, :], in_=s5t[i2][:, k * 128:(k + 1) * 128],
                                                     identity=ident_b[:, :]),
                         reads=[s5t[i2], ident_b], writes=[tp], inc=(k == 3))
                if not nos5:
                    K.op("act", lambda e: e.copy(out=cT[:, 0:4, :], in_=tp[:, :, :]), reads=[tp], writes=[cT])

            def stage2(n):
                sname, t0 = otiles[n]
                i2 = n % 4
                cT = catT[i2]
                ab = aTb[(t0 // 512) % 2]
                tsub = (t0 % 512)
                for nh in range(2):
                    for k in range(k0, 8):
                        lh = cT[:, k, :] if k < 4 else ab[:, k - 4, tsub:tsub + 128]
                        K.op("pe", lambda e: e.matmul(out=pp[nh][:, :], lhsT=lh,
                                                      rhs=wo[:, k, nh * 512:(nh + 1) * 512],
                                                      start=(k == k0), stop=(k == 7)),
                             reads=[cT, ab, wo], writes=[pp[nh]], inc=(k == 7))
                    K.op("dve", lambda e: e.tensor_tensor(out=xo[i2][:, nh * 512:(nh + 1) * 512], in0=pp[nh][:, :],
                                                          in1=xa[i2][:, nh * 512:(nh + 1) * 512], op=ALU.add),
                         reads=[pp[nh], xa[i2]], writes=[xo[i2]])
                K.dma("pool", x1[sname][t0:t0 + 128, :], xo[i2][:], reads=[xo[i2]])


            stage1(0)
            for n in range(len(otiles)):
                if n + 1 < len(otiles):
                    stage1(n + 1)
                stage2(n)

        def pool_phase(src, dst):
            pw = K.sb([128, 4, 2, 256], BF16, "pw")
            for g in range(4):
                K.dma("pool", pw[:, g, :, :], pool_w[0, g].rearrange("(k p) n -> p k n", p=128), writes=[pw])
            bd = K.sb([128, 20, 128], BF16, "bands")
            K.dma("pool", bd[:, :, :], bands_in.rearrange("w v j t -> j (w v) t"), writes=[bd])
            gam = load_rep(mix_norm[1, :], D, "gam_mix1")
            psc = load_rep(pool_scale[0, :], D, "pscale")
            xa = [K.sb([128, D], F32, "pl_x%d" % i) for i in range(6)]
            hTa = [K.sb([128, 8, 128], BF16, "pl_hT%d" % i) for i in range(3)]
            zt = [K.sb([128, D], BF16, "pl_z%d" % i) for i in range(4)]
            zp = [K.ps([128, 512], F32, "pl_zp%d" % i) for i in range(2)]
            op_ = [K.ps([128, 512], F32, "pl_op%d" % i) for i in range(2)]
            tmp = K.sb([128, D], F32, "pl_tmp")
            xo = [K.sb([128, D], F32, "pl_xo%d" % i) for i in range(3)]
            nn = 0
            for sname, L in seqs:
                NTt = L // 128

                def make_z(i):
                    x_t = xa[i % 6]
                    K.dma("sp", x_t[:], src[sname][i * 128:(i + 1) * 128, :], writes=[x_t])
                    hT_t = hTa[i % 3]
                    norm_to_T(x_t, 128, gam, hT_t, hT_t[:, :, :])
                    for g in range(4):
                        for k in range(2):
                            K.op("pe", lambda e: e.matmul(out=zp[g // 2][:, (g % 2) * 256:(g % 2) * 256 + 256],
                                                          lhsT=hT_t[:, 2 * g + k, :], rhs=pw[:, g, k, :],
                                                          start=(k == 0), stop=(k == 1)),
                                 reads=[hT_t, pw], writes=[zp[g // 2]], inc=(k == 1))
                    for hf in range(2):
                        K.op("act", lambda e: e.copy(out=zt[i % 4][:, hf * 512:(hf + 1) * 512], in_=zp[hf][:, :]),
                             reads=[zp[hf]], writes=[zt[i % 4]])

                make_z(0)
                make_z(1)
                for i in range(NTt):
                    if i + 2 < NTt:
                        make_z(i + 2)
                    for g in range(4):
                        terms = []
                        if i > 0:
                            terms.append((zt[(i - 1) % 4], 0))
                        terms.append((zt[i % 4], 3 if i == 0 else (4 if i == NTt - 1 else 2)))
                        if i + 1 < NTt:
                            terms.append((zt[(i + 1) % 4], 1))
                        for ti, (zsrc, var) in enumerate(terms):
                            K.op("pe", lambda e: e.matmul(out=op_[g // 2][:, (g % 2) * 256:(g % 2) * 256 + 256],
                                                          lhsT=bd[:, g * 5 + var, :], rhs=zsrc[:, g * 256:(g + 1) * 256],
                                                          start=(ti == 0), stop=(ti == len(terms) - 1)),
                                 reads=[bd, zsrc], writes=[op_[g // 2]], inc=(ti == len(terms) - 1))
                    xo_t = xo[nn % 3]
                    nn += 1
                    for hf in range(2):
                        K.op("dve", lambda e: e.tensor_tensor(out=tmp[:, hf * 512:(hf + 1) * 512], in0=op_[hf][:, :],
                                                              in1=psc[:, hf * 512:(hf + 1) * 512], op=ALU.mult),
                             reads=[op_[hf], psc], writes=[tmp])
                    K.op("dve", lambda e: e.tensor_tensor(out=xo_t[:, :], in0=tmp[:, :], in1=xa[i % 6][:, :], op=ALU.add),
                         reads=[tmp, xa[i % 6]], writes=[xo_t])
                    K.dma("pool", dst[sname][i * 128:(i + 1) * 128, :], xo_t[:], reads=[xo_t])

        if "ffn_only" in phases:
            K.run_phase(lambda: ffn_phase(0, xin, yout, True))
        if "full" in phases:
            K.run_phase(proj_phase)
            if "nos5" not in phases:
                s5_phase(attn_phase)
            else:
                K.run_phase(attn_phase)
            K.run_phase(outproj_phase)
            K.run_phase(lambda: ffn_phase(0, x1, x2, False))
            K.run_phase(lambda: pool_phase(x2, x3))
            K.run_phase(lambda: ffn_phase(1, x3, yout, True))
    return nc


PHASES = ("full",)


def kernel(**inputs):
    x_prompt = np.ascontiguousarray(np.asarray(inputs["x_prompt"], dtype=np.float32))
    x_sample = np.ascontiguousarray(np.asarray(inputs["x_sample"], dtype=np.float32))
    n = x_prompt.shape[0]
    LP, LS = x_prompt.shape[1], x_sample.shape[1]
    nc = bass.Bass("TRN2", target_bir_lowering=False)
    build_program(nc, LP, LS, phases=PHASES)
    shared = {k: np.ascontiguousarray(np.asarray(v, dtype=np.float32)) for k, v in inputs.items()
              if k not in ("x_prompt", "x_sample")}
    shared.update(host_consts(LP))
    in_maps = []
    for i in range(n):
        m = dict(shared)
        m["x_p"] = x_prompt[i]
        m["x_s"] = x_sample[i]
        in_maps.append(m)
    res = run_bass_kernel_spmd(nc, in_maps, core_ids=list(range(n)))
    y_p = np.stack([np.asarray(r["y_p"], dtype=np.float32) for r in res.results], 0)
    y_s = np.stack([np.asarray(r["y_s"], dtype=np.float32) for r in res.results], 0)
    return (y_p, y_s)


def _rope_tables(L):
    t = np.arange(L)
    row = (t // 64).astype(np.float32)
    col = (t % 64).astype(np.float32)
    inv = np.power(np.float32(10000.0), -np.arange(16, dtype=np.float32) / np.float32(16)).astype(np.float32)
    ar = (row[:, None] * inv).astype(np.float32)
    ac = (col[:, None] * inv).astype(np.float32)
    cos = np.stack([np.cos(ar), np.cos(ac)], 1).astype(np.float32)
    sin = np.stack([np.sin(ar), np.sin(ac)], 1).astype(np.float32)
    cos10 = np.ascontiguousarray(np.broadcast_to(cos[:, None], (L, 10, 2, 16)).reshape(L, 320))
    sin10 = np.ascontiguousarray(np.broadcast_to(sin[:, None], (L, 10, 2, 16)).reshape(L, 320))
    return cos10, sin10


def _bands():
    out = np.zeros((4, 5, 128, 128), np.float32)
    L = 384
    for g, w in enumerate((2, 4, 8, 16)):
        t = np.arange(L)
        lo = np.maximum(t - w // 2, 0)
        hi = np.minimum(t + (w - w // 2) - 1, L - 1)
        cnt = (hi - lo + 1).astype(np.float64)
        j = np.arange(L)[:, None]
        M = ((j >= lo[None, :]) & (j <= hi[None, :])) / cnt[None, :] - np.eye(L)
        out[g, 0] = M[0:128, 128:256]
        out[g, 1] = M[256:384, 128:256]
        out[g, 2] = M[128:256, 128:256]
        out[g, 3] = M[0:128, 0:128]
        out[g, 4] = M[256:384, 256:384]
    return out


def host_consts(LP):
    c, s = _rope_tables(LP)
    jj = np.arange(128)[:, None] // 16
    tt_ = np.arange(128)[None, :] // 16
    return dict(ident=np.eye(128, dtype=np.float32), rope_cos=c, rope_sin=s, bands=_bands(),
                maskf=(tt_ >= jj).astype(np.float32), maskb=(tt_ <= jj).astype(np.float32))
```

```python
from contextlib import ExitStack
import numpy as np
import concourse.bass as bass
import concourse.mybir as mybir
from concourse.bass_utils import run_bass_kernel_spmd

F32 = mybir.dt.float32
BF16 = mybir.dt.bfloat16
ALU = mybir.AluOpType
AF = mybir.ActivationFunctionType
AX = mybir.AxisListType

import os
S5CUT = int(os.environ.get("S5CUT", "0"))
D = 1024
FF = 2816
NFT = FF // 128
EPS = 1e-6
NDMA_SEM = 12


class Tl:
    def __init__(self, t, name):
        self.t = t
        self.name = name
        self.writes = {}
        self.reads = {}

    def __getitem__(self, idx):
        return self.t[idx]


class _Rec:
    def __getattr__(self, name):
        def f(*args, **kw):
            self.call = (name, args, kw)
        return f


class Sched:
    def __init__(self, nc, es):
        self.nc = nc
        self.es = es
        self.names = ["pe", "act", "dve", "pool", "sp"]
        self.prog = {e: [] for e in self.names}
        self.sem = {e: es.enter_context(nc.semaphore("s_" + e)) for e in self.names}
        self.cnt = {e: 0 for e in self.names}
        self.seen = {e: {} for e in self.names}
        self.dsem = {}
        self.dval = {}
        self.drr = {}
        for q in ("sp", "pool", "act"):
            self.dsem[q] = [es.enter_context(nc.semaphore("d_%s%d" % (q, i))) for i in range(NDMA_SEM)]
            self.dval[q] = [0] * NDMA_SEM
            self.drr[q] = 0
        self.ntile = 0

    def sb(self, shape, dt, name=None):
        self.ntile += 1
        name = "%s_%d" % (name or "t", self.ntile)
        return Tl(self.es.enter_context(self.nc.sbuf_tensor(name, list(shape), dt)), name)

    def ps(self, shape, dt, name=None):
        self.ntile += 1
        name = "%s_%d" % (name or "p", self.ntile)
        return Tl(self.es.enter_context(self.nc.psum_tensor(name, list(shape), dt)), name)

    def _deps(self, eng, reads, writes):
        deps = {}

        def add(d):
            for k, (s, v) in d.items():
                if k not in deps or deps[k][1] < v:
                    deps[k] = (s, v)

        for t in reads:
            add(t.writes)
        for t in writes:
            add(t.writes)
            add(t.reads)
        out = []
        for k, (s, v) in deps.items():
            if eng == "pe" and k == "pe":
                continue
            if self.seen[eng].get(k, 0) >= v:
                continue
            self.seen[eng][k] = v
            out.append((s, v))
        return out

    def _mark(self, reads, writes, key, ticket):
        for t in reads:
            t.reads[key] = ticket
        for t in writes:
            t.writes = {key: ticket}
            t.reads = {}

    def op(self, eng, fn, reads=(), writes=(), inc=True):
        waits = self._deps(eng, reads, writes)
        if inc:
            self.cnt[eng] += 1
            ticket = (self.sem[eng], self.cnt[eng])
        else:
            ticket = (self.sem[eng], self.cnt[eng] + 1)
        rec = _Rec()
        fn(rec)
        name, args, kw = rec.call
        self.prog[eng].append((waits, lambda e: getattr(e, name)(*args, **kw),
                               (self.sem[eng], 1) if inc else None))
        self._mark(reads, writes, eng, ticket)

    def dma(self, q, out, in_, reads=(), writes=(), **kw):
        i = self.drr[q]
        self.drr[q] = (i + 1) % NDMA_SEM
        sem = self.dsem[q][i]
        key = "d_%s%d" % (q, i)
        waits = self._deps(q, reads, writes)
        prev = self.dval[q][i]
        if prev > 0 and self.seen[q].get(key, 0) < prev:
            self.seen[q][key] = prev
            waits.append((sem, prev))
        self.dval[q][i] = prev + 16
        self.prog[q].append((waits, lambda e: e.dma_start(out=out, in_=in_, **kw), (sem, 16)))
        self._mark(reads, writes, key, (sem, prev + 16))

    def barrier(self):
        allw = [(self.sem[e], self.cnt[e], e) for e in self.names if self.cnt[e] > 0]
        for q in self.dsem:
            for i in range(NDMA_SEM):
                if self.dval[q][i] > 0:
                    allw.append((self.dsem[q][i], self.dval[q][i], "d_%s%d" % (q, i)))
        for e in self.names:
            waits = []
            for (s, v, k) in allw:
                if self.seen[e].get(k, 0) < v:
                    self.seen[e][k] = v
                    waits.append((s, v))
            if waits:
                self.prog[e].append((waits, None, None))

    def emit(self):
        nc = self.nc
        engs = {"pe": "tensor", "act": "scalar", "dve": "vector", "pool": "gpsimd", "sp": "sync"}
        with nc.Block() as block:
            for e in self.names:
                prog = self.prog[e]

                def body(eng, prog=prog):
                    for waits, fn, inc in prog:
                        for (s, v) in waits:
                            eng.wait_ge(s, v)
                        if fn is None:
                            continue
                        ins = fn(eng)
                        if inc is not None:
                            ins.then_inc(inc[0], inc[1])

                getattr(block, engs[e])(body)
        self.prog = {e: [] for e in self.names}

    def run_phase(self, fn):
        outer = self.es
        with ExitStack() as pes:
            self.es = pes
            fn()
            self.barrier()
            self.emit()
        self.es = outer


def build_program(nc, LP, LS, phases=("ffn0",), dbg=False):
    es = ExitStack()
    with es:
        K = Sched(nc, es)
        nc_es = es

        def din(name, shape, dt=F32):
            return nc.dram_tensor(name, list(shape), dt, kind="ExternalInput").ap()

        def dscr(name, shape, dt=F32):
            return nc.dram_tensor(name, list(shape), dt, kind="Internal").ap()

        seqs = [("p", LP), ("s", LS)]
        xin = {"p": din("x_p", [LP, D]), "s": din("x_s", [LS, D])}
        yout = {
            "p": nc.dram_tensor("y_p", [LP, D], F32, kind="ExternalOutput").ap(),
            "s": nc.dram_tensor("y_s", [LS, D], F32, kind="ExternalOutput").ap(),
        }
        ffn_norm = din("ffn_norm", [2, D])
        final_norm = din("final_norm", [D])
        w_up = din("ffn_w_up", [2, D, 2 * FF])
        conv_w = din("ffn_conv_w", [2, 3, 2 * FF])
        conv_b = din("ffn_conv_b", [2, 2 * FF])
        w_down = din("ffn_w_down", [2, FF, D])
        ident_in = din("ident", [128, 128])

        ident_f = K.sb([128, 128], F32, "ident_f")
        ident_b = K.sb([128, 128], BF16, "ident_b")
        K.dma("sp", ident_f[:], ident_in, writes=[ident_f])
        K.op("dve", lambda e: e.tensor_copy(out=ident_b[:], in_=ident_f[:]), reads=[ident_f], writes=[ident_b])

        def load_rep(vec_ap, n, name):
            t = K.sb([128, n], F32, name)
            K.dma("sp", t[:], vec_ap.partition_broadcast(128), writes=[t])
            return t

        def run_interleaved(gens, lead=None):
            gens = list(gens)
            for g, n_ in zip(gens, lead or []):
                for _ in range(n_):
                    next(g)
            while gens:
                for g in list(gens):
                    try:
                        next(g)
                    except StopIteration:
                        gens.remove(g)

        NXB = 3
        st_small = [K.sb([128, 4], F32, "st%d" % i) for i in range(NXB)]
        hb_tiles = [K.sb([128, D], BF16, "hb%d" % i) for i in range(NXB)]
        tp_hold = {}

        def tp_get():
            if tp_hold.get("es") is not K.es:
                tp_hold["es"] = K.es
                tp_hold["ps"] = [K.ps([128, 8, 128], BF16, "tp_ps%d" % i) for i in range(2)]
            return tp_hold["ps"]
        ctr = {"n": 0, "tp": 0}

        neghalf = K.sb([128, 1], F32, "neghalf")
        K.op("pool", lambda e: e.memset(neghalf[:, :], -0.5), writes=[neghalf])

        def rstd_ops(st, rows, n=D):
            K.op("pool", lambda e: e.tensor_scalar(out=st[:rows, 1:2], in0=st[:rows, 0:1], scalar1=1.0 / n,
                                                   scalar2=EPS, op0=ALU.mult, op1=ALU.add),
                 reads=[st], writes=[st])
            K.op("pool", lambda e: e.tensor_tensor(out=st[:rows, 2:3], in0=st[:rows, 1:2], in1=neghalf[:rows, 0:1],
                                                   op=ALU.pow), reads=[st, neghalf], writes=[st])

        def norm_part(x_t, rows, gamma_rep):
            i = ctr["n"] % NXB
            ctr["n"] += 1
            st = st_small[i]
            hb = hb_tiles[i]
            K.op("act", lambda e: e.activation(out=hb[:rows, :], in_=x_t[:rows, :], func=AF.Square,
                                               accum_out=st[:rows, 0:1]),
                 reads=[x_t], writes=[hb, st])
            rstd_ops(st, rows)
            K.op("dve", lambda e: e.scalar_tensor_tensor(out=hb[:rows, :], in0=x_t[:rows, :],
                                                         scalar=st[:rows, 2:3], in1=gamma_rep[:rows, :],
                                                         op0=ALU.mult, op1=ALU.mult),
                 reads=[x_t, st, gamma_rep], writes=[hb])
            return hb

        def T_part(hb, rows, dst_tile, dst_ap):
            j = ctr["tp"] % 2
            ctr["tp"] += 1
            tp = tp_get()[j]
            for k in range(8):
                K.op("pe", lambda e, k=k: e.transpose(out=tp[:, k, :rows], in_=hb[:rows, k * 128:(k + 1) * 128],
                                                      identity=ident_b[:rows, :rows]),
                     reads=[hb, ident_b], writes=[tp], inc=(k == 7))
            K.op("act", lambda e: e.copy(out=dst_ap, in_=tp[:, :, :rows]), reads=[tp], writes=[dst_tile])

        def norm_to_T(x_t, rows, gamma_rep, dst_tile, dst_ap):
            T_part(norm_part(x_t, rows, gamma_rep), rows, dst_tile, dst_ap)

        def ffn_phase(layer, src, dst, final):
            TB = 256
            wup = K.sb([128, 8, 2 * FF], BF16, "wup")
            wdn = K.sb([128, NFT, D], BF16, "wdn")
            for k in range(8):
                K.dma("pool", wup[:, k, :], w_up[layer, k * 128:(k + 1) * 128, :], writes=[wup])
            for j in range(NFT):
                K.dma("pool", wdn[:, j, :], w_down[layer, j * 128:(j + 1) * 128, :], writes=[wdn])
            cw = K.sb([128, 44, 3], F32, "cw")
            cb = K.sb([128, 44], F32, "cb")
            for d3 in range(3):
                K.dma("sp", cw[:, :, d3], conv_w[layer, d3, :].rearrange("(j p) -> p j", p=128), writes=[cw],
                      allow_slow_non_contiguous=True)
            K.dma("sp", cb[:], conv_b[layer, :].rearrange("(j p) -> p j", p=128), writes=[cb],
                  allow_slow_non_contiguous=True)
            gam = load_rep(ffn_norm[layer, :], D, "gam_ffn")
            gfin = load_rep(final_norm, D, "gam_fin") if final else None

            NB = 2
            xt = [[K.sb([128, D], F32, "fx%d_%d" % (b, i)) for i in range(2)] for b in range(NB)]
            hT = [K.sb([128, 8, TB + 2], BF16, "fhT%d" % b) for b in range(NB)]
            actT = [K.sb([128, NFT, TB], BF16, "factT%d" % b) for b in range(NB)]
            up_ps = [K.ps([128, 512], F32, "up_ps%d" % i) for i in range(4)]
            dn_ps = [K.ps([128, 512], F32, "dn_ps%d" % i) for i in range(2)]
            cg = [K.sb([128, TB], F32, "cg%d" % i) for i in range(2)]
            cv = [K.sb([128, TB], F32, "cv%d" % i) for i in range(2)]
            sg = [K.sb([128, TB], F32, "sg%d" % i) for i in range(2)]
            xo = [K.sb([128, D], F32, "fxo%d" % i) for i in range(2)]
            xh = [xo[0]] * NB
            yo = None
            fst = [K.sb([128, 4], F32, "fst%d" % i) for i in range(2)] if final else None
            cnt = {"blk": 0, "up": 0, "dn": 0, "c": 0, "xo": 0}

            blocks = [(sname, L, t0) for sname, L in seqs for t0 in range(0, L, TB)]

            pend_T = {}

            def prep_norm(bi):
                sname, L, t0 = blocks[bi]
                xs = src[sname]
                b = bi % NB
                for i in range(2):
                    K.dma("sp", xt[b][i][:], xs[t0 + i * 128:t0 + (i + 1) * 128, :], writes=[xt[b][i]])
                hTb = hT[b]
                todo = []
                lo_ok = t0 > 0
                hi_ok = t0 + TB < L
                if lo_ok and hi_ok:
                    K.dma("sp", xh[b][0:1, :], xs[t0 - 1:t0, :], writes=[xh[b]])
                    K.dma("sp", xh[b][1:2, :], xs[t0 + TB:t0 + TB + 1, :], writes=[xh[b]])
                    todo.append((norm_part(xh[b], 2, gam), 2, hTb[:, :, bass.ds(0, 2, step=TB + 1)]))
                else:
                    K.op("pool", lambda e, hTb=hTb: e.memset(hTb[:, :, bass.ds(0, 2, step=TB + 1)], 0.0),
                         writes=[hTb])
                    if lo_ok:
                        K.dma("sp", xh[b][0:1, :], xs[t0 - 1:t0, :], writes=[xh[b]])
                        todo.append((norm_part(xh[b], 1, gam), 1, hTb[:, :, 0:1]))
                    if hi_ok:
                        K.dma("sp", xh[b][0:1, :], xs[t0 + TB:t0 + TB + 1, :], writes=[xh[b]])
                        todo.append((norm_part(xh[b], 1, gam), 1, hTb[:, :, TB + 1:TB + 2]))
                for i in range(2):
                    todo.append((norm_part(xt[b][i], 128, gam), 128, hTb[:, :, 1 + i * 128:1 + (i + 1) * 128]))
                pend_T[bi] = todo

            def prep_T(bi):
                hTb = hT[bi % NB]
                for hb, rows, dst in pend_T.pop(bi):
                    T_part(hb, rows, hTb, dst)

            def prep(bi):
                prep_norm(bi)
                prep_T(bi)

            def up_part(bi, j0, j1):
                sname, L, t0 = blocks[bi]
                xd = dst[sname]
                b = bi % NB
                hTb = hT[b]
                aT = actT[b]
                for j in range(j0, j1):
                    pss = []
                    for half in range(2):
                        ct = j + half * NFT
                        ps = up_ps[cnt["up"] % 4]
                        cnt["up"] += 1
                        for k in range(8):
                            K.op("pe", lambda e, ps=ps, k=k, ct=ct: e.matmul(
                                out=ps[:, 0:TB + 2], lhsT=wup[:, k, ct * 128:(ct + 1) * 128],
                                rhs=hTb[:, k, :], start=(k == 0), stop=(k == 7)),
                                reads=[wup, hTb], writes=[ps], inc=(k == 7))
                        pss.append((ps, ct))
                    ci = cnt["c"] % 2
                    cnt["c"] += 1
                    outs = []
                    for (ps, ct), ctile in zip(pss, (cg[ci], cv[ci])):
                        K.op("act", lambda e, ps=ps, ct=ct, ctile=ctile: e.activation(
                            out=ctile[:], in_=ps[:, 1:TB + 1], func=AF.Identity,
                            bias=cb[:, ct:ct + 1], scale=cw[:, ct, 1:2]),
                            reads=[ps, cb, cw], writes=[ctile])
                        K.op("dve", lambda e, ps=ps, ct=ct, ctile=ctile: e.scalar_tensor_tensor(
                            out=ctile[:], in0=ps[:, 0:TB], scalar=cw[:, ct, 0:1], in1=ctile[:],
                            op0=ALU.mult, op1=ALU.add),
                            reads=[ps, cw, ctile], writes=[ctile])
                        K.op("dve", lambda e, ps=ps, ct=ct, ctile=ctile: e.scalar_tensor_tensor(
                            out=ctile[:], in0=ps[:, 2:TB + 2], scalar=cw[:, ct, 2:3], in1=ctile[:],
                            op0=ALU.mult, op1=ALU.add),
                            reads=[ps, cw, ctile], writes=[ctile])
                    sgt = sg[ci]
                    K.op("act", lambda e, ci=ci, sgt=sgt: e.activation(out=sgt[:], in_=cg[ci][:], func=AF.Silu),
                         reads=[cg[ci]], writes=[sgt])
                    K.op("pool", lambda e, ci=ci, sgt=sgt, j=j, aT=aT: e.tensor_tensor(
                        out=aT[:, j, :], in0=sgt[:], in1=cv[ci][:], op=ALU.mult),
                        reads=[sgt, cv[ci]], writes=[aT])
            def down_part(bi):
                sname, L, t0 = blocks[bi]
                xd = dst[sname]
                b = bi % NB
                aT = actT[b]
                for i in range(2):
                    xoi = xo[cnt["xo"] % 2]
                    yoi = xt[b][i] if final else None
                    fsti = fst[cnt["xo"] % 2] if final else None
                    cnt["xo"] += 1
                    for nh in range(2):
                        ps = dn_ps[cnt["dn"] % 2]
                        cnt["dn"] += 1
                        for j in range(NFT):
                            K.op("pe", lambda e, ps=ps, j=j, i=i, nh=nh: e.matmul(
                                out=ps[:, :], lhsT=aT[:, j, i * 128:(i + 1) * 128],
                                rhs=wdn[:, j, nh * 512:(nh + 1) * 512], start=(j == 0), stop=(j == NFT - 1)),
                                reads=[aT, wdn], writes=[ps], inc=(j == NFT - 1))
                        K.op("dve", lambda e, ps=ps, i=i, nh=nh, xoi=xoi: e.tensor_tensor(
                            out=xoi[:, nh * 512:(nh + 1) * 512], in0=ps[:, :],
                            in1=xt[b][i][:, nh * 512:(nh + 1) * 512], op=ALU.add),
                            reads=[ps, xt[b][i]], writes=[xoi])
                    if not final:
                        K.dma("pool", xd[t0 + i * 128:t0 + (i + 1) * 128, :], xoi[:], reads=[xoi])
                    else:
                        K.op("act", lambda e, xoi=xoi, fsti=fsti: e.activation(
                            out=yoi[:, :], in_=xoi[:, :], func=AF.Square, accum_out=fsti[:, 0:1]),
                            reads=[xoi], writes=[yoi, fsti])
                        rstd_ops(fsti, 128)
                        K.op("dve", lambda e, xoi=xoi, yoi=yoi, fsti=fsti: e.scalar_tensor_tensor(
                            out=yoi[:, :], in0=xoi[:, :], scalar=fsti[:, 2:3], in1=gfin[:, :],
                            op0=ALU.mult, op1=ALU.mult), reads=[xoi, fsti, gfin], writes=[yoi])
                        K.dma("pool", xd[t0 + i * 128:t0 + (i + 1) * 128, :], yoi[:], reads=[yoi])


            prep(0)
            JS, PN, JS2 = 4, 9, 16
            for bi in range(len(blocks)):
                up_part(bi, 0, JS)
                if bi > 0:
                    down_part(bi - 1)
                up_part(bi, JS, PN)
                if bi + 1 < len(blocks):
                    prep_norm(bi + 1)
                up_part(bi, PN, JS2)
                if bi + 1 < len(blocks):
                    prep_T(bi + 1)
                up_part(bi, JS2, NFT)
            down_part(len(blocks) - 1)

        mix_norm = din("mix_norm", [2, D])
        w_in = din("w_in", [1, D, 1280])
        q_norm = din("q_norm", [1, 64])
        k_norm = din("k_norm", [1, 64])
        w_out = din("w_out", [1, D, D])
        pool_w = din("pool_w", [1, 4, 256, 256])
        pool_scale = din("pool_scale", [1, D])
        rope_cos = din("rope_cos", [LP, 320])
        rope_sin = din("rope_sin", [LP, 320])
        bands_in = din("bands", [4, 5, 128, 128])
        x1 = {s: dscr("x1_" + s, [L, D]) for s, L in seqs}
        x2 = {s: dscr("x2_" + s, [L, D]) for s, L in seqs}
        x3 = {s: dscr("x3_" + s, [L, D]) for s, L in seqs}
        u_d = {s: dscr("u_" + s, [L, 512], BF16) for s, L in seqs}
        if dbg:
            s5o_d = {s: nc.dram_tensor("s5o_" + s, [L, 512], BF16, kind="ExternalOutput").ap() for s, L in seqs}
        else:
            s5o_d = {s: dscr("s5o_" + s, [L, 512], BF16) for s, L in seqs}
        qT_d = {s: dscr("qT_" + s, [4, 128, L], BF16) for s, L in seqs}
        kT_d = {s: dscr("kT_" + s, [128, L], BF16) for s, L in seqs}
        v_d = {s: dscr("v_" + s, [L, 128], BF16) for s, L in seqs}
        aT_d = {s: dscr("aT_" + s, [512, L], BF16) for s, L in seqs}
        den2_d = {s: dscr("den_" + s, [8, L], F32) for s, L in seqs}

        def proj_phase():
            win = K.sb([128, 8, 1280], BF16, "win")
            for k in range(8):
                K.dma("pool", win[:, k, 0:512], w_in[0, k * 128:(k + 1) * 128, 0:512], writes=[win])
                K.dma("pool", win[:, k, 1024:1280], w_in[0, k * 128:(k + 1) * 128, 1024:1280], writes=[win])
            for h in range(8):
                slot = 2 * h if h < 4 else 2 * (h - 4) + 1
                K.dma("pool", win[:, :, 512 + slot * 64:512 + (slot + 1) * 64],
                      w_in[0, :, 512 + h * 64:512 + (h + 1) * 64].rearrange("(k p) d -> p k d", p=128), writes=[win])
            gam = load_rep(mix_norm[0, :], D, "gam_mix0")
            gain = K.sb([128, 10, 64], F32, "gain10")
            for h in range(10):
                K.dma("sp", gain[:, h, :], (q_norm if h < 8 else k_norm)[0, :].partition_broadcast(128),
                      writes=[gain])
            xa = [K.sb([128, D], F32, "pa_x%d" % i) for i in range(2)]
            hTa = [K.sb([128, 8, 128], BF16, "pa_hT%d" % i) for i in range(2)]
            pp_u = [K.ps([128, 512], F32, "pa_pu%d" % i) for i in range(2)]
            pp_q = [K.ps([128, 512], F32, "pa_pq%d" % i) for i in range(2)]
            kvb = K.ps([128, 512], F32, "pa_pkv")
            kv_h = [kvb, Tl(kvb.t, "pa_pkv_b")]
            tq = K.ps([128, 5, 128], BF16, "pa_tq")
            sqt_l = [K.sb([128, 640], F32, "pa_sq%d" % i) for i in range(2)]
            ss_l = [K.sb([128, 40], F32, "pa_ss%d" % i) for i in range(2)]
            qk_l = [K.sb([128, 640], F32, "pa_qk%d" % i) for i in range(2)]
            qkr_l = [K.sb([128, 640], BF16, "pa_qkr%d" % i) for i in range(2)]
            t1_l = [K.sb([128, 320], F32, "pa_t1%d" % i) for i in range(2)]
            t2_l = [K.sb([128, 320], F32, "pa_t2%d" % i) for i in range(2)]
            cs = [K.sb([128, 320], F32, "pa_cos%d" % i) for i in range(2)]
            sn = [K.sb([128, 320], F32, "pa_sin%d" % i) for i in range(2)]
            ublk = [K.sb([128, 4, 512], BF16, "pa_u%d" % i) for i in range(2)]
            vblk = [K.sb([128, 4, 128], BF16, "pa_v%d" % i) for i in range(2)]
            qTb = [K.sb([128, 5, 512], BF16, "pa_qT%d" % i) for i in range(2)]
            tiles = [(sname, L, b0, ti) for sname, L in seqs for b0 in range(0, L, 512) for ti in range(4)]
            blk_of = {}
            for idx, (sname, L, b0, ti) in enumerate(tiles):
                blk_of[idx] = idx // 4

            def stage_a(idx):
                sname, L, b0, ti = tiles[idx]
                bi = blk_of[idx] % 2
                i2 = idx % 2
                t0 = b0 + ti * 128
                x_t = xa[i2]
                sqt, ss, qk, qkr, t1, t2 = sqt_l[i2], ss_l[i2], qk_l[i2], qkr_l[i2], t1_l[i2], t2_l[i2]
                pu, pq, kvt, ko = pp_u[i2], pp_q[i2], kv_h[i2], i2 * 256
                pouts = [(pu, pu[:, 0:512]), (pq, pq[:, 0:512]), (kvt, kvt[:, ko:ko + 256])]
                K.dma("sp", x_t[:], xin[sname][t0:t0 + 128, :], writes=[x_t])
                K.dma("sp", cs[i2][:], rope_cos[t0:t0 + 128, :], writes=[cs[i2]])
                K.dma("sp", sn[i2][:], rope_sin[t0:t0 + 128, :], writes=[sn[i2]])
                hT_t = hTa[i2]
                norm_to_T(x_t, 128, gam, hT_t, hT_t[:, :, :])
                yield
                for c3, (n0, nn) in enumerate(((0, 512), (512, 512), (1024, 256))):
                    for k in range(8):
                        K.op("pe", lambda e: e.matmul(out=pouts[c3][1], lhsT=hT_t[:, k, :],
                                                      rhs=win[:, k, n0:n0 + nn], start=(k == 0), stop=(k == 7)),
                             reads=[hT_t, win], writes=[pouts[c3][0]], inc=(k == 7))
                K.op("act", lambda e: e.copy(out=ublk[bi][:, ti, :], in_=pu[:, :]),
                     reads=[pu], writes=[ublk[bi]])
                K.op("act", lambda e: e.copy(out=vblk[bi][:, ti, :], in_=kvt[:, ko + 128:ko + 256]),
                     reads=[kvt], writes=[vblk[bi]])
                K.op("act", lambda e: e.activation(out=sqt[:, 0:512], in_=pq[:, :], func=AF.Square),
                     reads=[pq], writes=[sqt])
                K.op("act", lambda e: e.activation(out=sqt[:, 512:640], in_=kvt[:, ko:ko + 128], func=AF.Square),
                     reads=[kvt], writes=[sqt])
                yield
                K.op("dve", lambda e: e.tensor_reduce(out=ss[:, 0:10],
                                                      in_=sqt[:, :].rearrange("p (h d) -> p h d", d=64),
                                                      axis=AX.X, op=ALU.add), reads=[sqt], writes=[ss])
                K.op("pool", lambda e: e.tensor_scalar(out=ss[:, 10:20], in0=ss[:, 0:10], scalar1=1.0 / 64,
                                                       scalar2=EPS, op0=ALU.mult, op1=ALU.add),
                     reads=[ss], writes=[ss])
                K.op("pool", lambda e: e.tensor_tensor(out=ss[:, 30:40], in0=ss[:, 10:20],
                                                       in1=neghalf[:, 0:1].to_broadcast([128, 10]), op=ALU.pow),
                     reads=[ss, neghalf], writes=[ss])
                qk3 = qk[:, :].rearrange("p (h d) -> p h d", d=64)
                K.op("dve", lambda e: e.tensor_tensor(
                    out=qk3[:, 0:8, :], in0=pq[:, :].rearrange("p (h d) -> p h d", d=64),
                    in1=ss[:, 30:38].unsqueeze(2).to_broadcast([128, 8, 64]), op=ALU.mult),
                    reads=[pq, ss], writes=[qk])
                K.op("dve", lambda e: e.tensor_tensor(
                    out=qk3[:, 8:10, :], in0=kvt[:, ko:ko + 128].rearrange("p (h d) -> p h d", d=64),
                    in1=ss[:, 38:40].unsqueeze(2).to_broadcast([128, 2, 64]), op=ALU.mult),
                    reads=[kvt, ss], writes=[qk])
                K.op("dve", lambda e: e.tensor_tensor(out=qk3, in0=qk3, in1=gain[:, :, :], op=ALU.mult),
                     reads=[qk, gain], writes=[qk])
                qv = qk[:, :].rearrange("p (g f i) -> p g f i", f=2, i=16)
                ov = qkr[:, :].rearrange("p (g f i) -> p g f i", f=2, i=16)
                cv_ = cs[i2][:, :].rearrange("p (g i) -> p g i", i=16)
                sv_ = sn[i2][:, :].rearrange("p (g i) -> p g i", i=16)
                t1v = t1[:, :].rearrange("p (g i) -> p g i", i=16)
                t2v = t2[:, :].rearrange("p (g i) -> p g i", i=16)
                K.op("dve", lambda e: e.tensor_tensor(out=t1v, in0=qv[:, :, 0, :], in1=cv_, op=ALU.mult),
                     reads=[qk, cs[i2]], writes=[t1])
                K.op("pool", lambda e: e.tensor_tensor(out=t2v, in0=qv[:, :, 1, :], in1=sv_, op=ALU.mult),
                     reads=[qk, sn[i2]], writes=[t2])
                K.op("dve", lambda e: e.tensor_tensor(out=ov[:, :, 0, :], in0=t1v, in1=t2v, op=ALU.subtract),
                     reads=[t1, t2], writes=[qkr])
                K.op("dve", lambda e: e.tensor_tensor(out=t1v, in0=qv[:, :, 0, :], in1=sv_, op=ALU.mult),
                     reads=[qk, sn[i2]], writes=[t1])
                K.op("pool", lambda e: e.tensor_tensor(out=t2v, in0=qv[:, :, 1, :], in1=cv_, op=ALU.mult),
                     reads=[qk, cs[i2]], writes=[t2])
                K.op("dve", lambda e: e.tensor_tensor(out=ov[:, :, 1, :], in0=t1v, in1=t2v, op=ALU.add),
                     reads=[t1, t2], writes=[qkr])

            def stage_b(idx):
                sname, L, b0, ti = tiles[idx]
                bi = blk_of[idx] % 2
                i2 = idx % 2
                qkr = qkr_l[i2]
                qr3 = qkr[:, :].rearrange("p (h d) -> p h d", d=64)
                for pi in range(5):
                    src = qkr[:, pi * 128:(pi + 1) * 128]
                    K.op("pe", lambda e: e.transpose(out=tq[:, pi, :], in_=src, identity=ident_b[:, :]),
                         reads=[qkr, ident_b], writes=[tq], inc=(pi == 4))
                K.op("act", lambda e: e.copy(out=qTb[bi][:, :, ti * 128:(ti + 1) * 128], in_=tq[:, :, :]),
                     reads=[tq], writes=[qTb[bi]])
                if ti == 3:
                    K.dma("pool", u_d[sname][b0:b0 + 512, :].rearrange("(t p) c -> p t c", p=128), ublk[bi][:, :, :],
                          reads=[ublk[bi]])
                    K.dma("pool", v_d[sname][b0:b0 + 512, :].rearrange("(t p) c -> p t c", p=128), vblk[bi][:, :, :],
                          reads=[vblk[bi]])
                    K.dma("pool", qT_d[sname][:, :, b0:b0 + 512].rearrange("i p t -> p i t"), qTb[bi][:, 0:4, :],
                          reads=[qTb[bi]])
                    K.dma("pool", kT_d[sname][:, b0:b0 + 512], qTb[bi][:, 4, :], reads=[qTb[bi]])


            def tile_gen(par):
                for idx in range(par, len(tiles), 2):
                    yield from stage_a(idx)
                    yield
                    stage_b(idx)
                    yield

            run_interleaved([tile_gen(0), tile_gen(1)], lead=[1, 0])

        def attn_phase():
            gq = load_rep(q_norm[0, :], 64, "gq_rep")
            gk = load_rep(k_norm[0, :], 64, "gk_rep")
            mm_ = K.sb([128, 4], F32, "negm")
            K.op("dve", lambda e: e.tensor_reduce(out=mm_[:, 0:1], in_=gq[:, :], axis=AX.X, op=ALU.max, apply_absolute_value=True),
                 reads=[gq], writes=[mm_])
            K.op("dve", lambda e: e.tensor_reduce(out=mm_[:, 1:2], in_=gk[:, :], axis=AX.X, op=ALU.max, apply_absolute_value=True),
                 reads=[gk], writes=[mm_])
            K.op("dve", lambda e: e.tensor_tensor(out=mm_[:, 2:3], in0=mm_[:, 0:1], in1=mm_[:, 1:2], op=ALU.mult),
                 reads=[mm_], writes=[mm_])
            K.op("dve", lambda e: e.tensor_scalar(out=mm_[:, 3:4], in0=mm_[:, 2:3], scalar1=-8.0, scalar2=None,
                                                  op0=ALU.mult), reads=[mm_], writes=[mm_])
            sps = [K.ps([128, 2, 512], F32, "at_s%d" % i) for i in range(3)]
            ops2 = [[K.ps([128, 512], F32, "at_o%d_%d" % (i, j)) for j in range(2)] for i in range(1)] * 2
            dns = [K.sb([65, 512], F32, "at_dn%d" % i) for i in range(2)]
            pT = [K.sb([128, 2, 512], BF16, "at_p%d" % i) for i in range(4)]
            qTt = [K.sb([128, 512], BF16, "at_q%d" % i) for i in range(2)]
            ao = [K.sb([64, 512], BF16, "at_ao%d" % i) for i in range(2)]
            ns = 0
            nq = 0
            no = 0
            for sname, L in seqs:
                NT = L // 128
                kT = K.sb([128, L], BF16, "at_kT_" + sname)
                K.dma("sp", kT[:, :], kT_d[sname][:, :], writes=[kT])
                va = K.sb([128, NT, 2, 65], BF16, "at_va_" + sname)
                K.op("pool", lambda e: e.memset(va[:, :, :, 64:65], 1.0), writes=[va])
                for hv in range(2):
                    K.dma("sp", va[:, :, hv, 0:64],
                          v_d[sname][:, hv * 64:(hv + 1) * 64].rearrange("(t p) d -> p t d", p=128), writes=[va])
                for q0 in range(0, L, 512):
                    for pi in range(4):
                        qt = qTt[nq % 2]
                        ops = ops2[nq % 2]
                        nq += 1
                        K.dma("sp", qt[:, :], qT_d[sname][pi, :, q0:q0 + 512], writes=[qt])
                        pendq = []
                        for kt in range(NT + 2):
                            if kt < NT:
                                s_ps = sps[ns % 3]
                                p_sb = pT[ns % 4]
                                ns += 1
                                for hh in range(2):
                                    r0 = hh * 64
                                    K.op("pe", lambda e: e.matmul(out=s_ps[:, hh, :], lhsT=kT[r0:r0 + 64, kt * 128:(kt + 1) * 128],
                                                                  rhs=qt[r0:r0 + 64, :], start=True, stop=True),
                                         reads=[kT, qt], writes=[s_ps], inc=(hh == 1))
                                K.op("act", lambda e: e.activation(out=p_sb[:, :, :], in_=s_ps[:, :, :], func=AF.Exp,
                                                                   bias=mm_[:, 3:4], scale=0.125),
                                     reads=[s_ps, mm_], writes=[p_sb])
                            if kt < NT:
                                pendq.append((kt, p_sb))
                            if len(pendq) > 2 or (kt >= NT and pendq):
                                pk, pp_sb = pendq.pop(0)
                                for hh in range(2):
                                    K.op("pe", lambda e: e.matmul(out=ops[hh][0:65, :], lhsT=va[:, pk, hh, :],
                                                                  rhs=pp_sb[:, hh, :], start=(pk == 0), stop=(pk == NT - 1)),
                                         reads=[va, pp_sb], writes=[ops[hh]], inc=(pk == NT - 1))
                        for hh in range(2):
                            h = pi + 4 * hh
                            o_ps = ops[hh]
                            a_o = ao[no % 2]
                            dn = dns[no % 2]
                            no += 1
                            K.op("act", lambda e: e.copy(out=a_o[:, :], in_=o_ps[0:64, :]), reads=[o_ps], writes=[a_o])
                            K.op("act", lambda e: e.copy(out=dn[64:65, :], in_=o_ps[64:65, :]), reads=[o_ps], writes=[dn])
                            K.dma("sp", aT_d[sname][h * 64:(h + 1) * 64, q0:q0 + 512], a_o[:, :], reads=[a_o])
                            K.dma("sp", den2_d[sname][h:h + 1, q0:q0 + 512], dn[64:65, :], reads=[dn])

        s5p = {nm: din(nm, shp) for nm, shp in (
            ("s5_lambda_re", [1, 2, 32, 64]), ("s5_lambda_im", [1, 2, 32, 64]), ("s5_log_dt", [1, 2, 32]),
            ("s5_b_re", [1, 2, 32, 64, 16]), ("s5_b_im", [1, 2, 32, 64, 16]),
            ("s5_c_re", [1, 2, 32, 16, 64]), ("s5_c_im", [1, 2, 32, 16, 64]),
            ("s5_d", [1, 512]), ("s5_w_glu", [1, 512, 512]), ("s5_b_glu", [1, 512]))}
        maskf_in = din("maskf", [128, 128])
        maskb_in = din("maskb", [128, 128])
        Hf_d = {s: dscr("Hf_" + s, [64, 32, 2, L // 8 + 1], BF16) for s, L in seqs}
        Hb_d = {s: dscr("Hb_" + s, [64, 32, 2, L // 8 + 1], BF16) for s, L in seqs}
        S_all_d = {s: dscr("Sall_" + s, [L // 1024, 128, 32 * 2 * 128], F32) for s, L in seqs}
        PI = float(np.pi)

        def s5_phase(attn_fn):
            s5es = ExitStack()
            outer = K.es
            K.es = s5es
            Bc = K.sb([128, 2, 32, 128], BF16, "s5_Bc")
            CcRb = K.sb([128, 32, 128], BF16, "s5_CcRb")
            CcIb = K.sb([128, 32, 128], BF16, "s5_CcIb")
            Wb = K.sb([128, 32, 128], BF16, "s5_W")
            A8 = K.sb([128, 3, 32], F32, "s5_A8")
            wglu = K.sb([128, 4, 512], BF16, "s5_wglu")
            bglu = K.sb([128, 512], F32, "s5_bglu")
            K.es = outer

            def tt(eng, out, in0, in1, op, reads, writes):
                K.op(eng, lambda e: e.tensor_tensor(out=out, in0=in0, in1=in1, op=op), reads=reads, writes=writes)

            def setup():
                K.dma("pool", wglu[:, :, :], s5p["s5_w_glu"][0].rearrange("(k p) n -> p k n", p=128), writes=[wglu])
                K.dma("sp", bglu[:, :], s5p["s5_b_glu"][0, :].partition_broadcast(128), writes=[bglu])
                lr = K.sb([128, 32], F32, "lr")
                li = K.sb([128, 32], F32, "li")
                ldt = K.sb([128, 32], F32, "ldt")
                Br = K.sb([128, 32, 16], F32, "Br")
                Bi = K.sb([128, 32, 16], F32, "Bi")
                Cr = K.sb([128, 32, 16], F32, "Cr")
                Ci = K.sb([128, 32, 16], F32, "Ci")
                for d in range(2):
                    ps_ = slice(d * 64, (d + 1) * 64)
                    K.dma("sp", lr[ps_, :], s5p["s5_lambda_re"][0, d].rearrange("g p -> p g"), writes=[lr],
                          allow_slow_non_contiguous=True)
                    K.dma("sp", li[ps_, :], s5p["s5_lambda_im"][0, d].rearrange("g p -> p g"), writes=[li],
                          allow_slow_non_contiguous=True)
                    K.dma("sp", ldt[ps_, :], s5p["s5_log_dt"][0, d, :].partition_broadcast(64), writes=[ldt])
                    K.dma("sp", Br[ps_, :, :], s5p["s5_b_re"][0, d].rearrange("g p c -> p g c"), writes=[Br])
                    K.dma("sp", Bi[ps_, :, :], s5p["s5_b_im"][0, d].rearrange("g p c -> p g c"), writes=[Bi])
                    for g4 in range(4):
                        gs = slice(g4 * 8, (g4 + 1) * 8)
                        K.dma("sp", Cr[ps_, gs, :], s5p["s5_c_re"][0, d, gs].rearrange("g c p -> p g c"), writes=[Cr],
                              allow_slow_non_contiguous=True)
                        K.dma("sp", Ci[ps_, gs, :], s5p["s5_c_im"][0, d, gs].rearrange("g c p -> p g c"), writes=[Ci],
                              allow_slow_non_contiguous=True)
                maskf = K.sb([128, 128], F32, "maskf")
                maskb = K.sb([128, 128], F32, "maskb")
                K.dma("sp", maskf[:, :], maskf_in, writes=[maskf])
                K.dma("sp", maskb[:, :], maskb_in, writes=[maskb])
                dcol = K.sb([128, 32], F32, "dcol")
                for j in range(8):
                    K.dma("sp", dcol[j * 16:(j + 1) * 16, :], s5p["s5_d"][0, :].rearrange("(g c) -> c g", c=16),
                          writes=[dcol], allow_slow_non_contiguous=True)
                cst = K.sb([128, 2], F32, "cst")
                if S5CUT == 1:
                    return
                K.op("pool", lambda e: e.memset(cst[:, 0:1], -PI), writes=[cst])
                dt = K.sb([128, 32], F32, "dt")
                K.op("act", lambda e: e.activation(out=dt[:, :], in_=ldt[:, :], func=AF.Exp), reads=[ldt], writes=[dt])
                lrdt = K.sb([128, 32], F32, "lrdt")
                lidt = K.sb([128, 32], F32, "lidt")
                tt("dve", lrdt[:, :], lr[:, :], dt[:, :], ALU.mult, [lr, dt], [lrdt])
                tt("dve", lidt[:, :], li[:, :], dt[:, :], ALU.mult, [li, dt], [lidt])
                ang = K.sb([128, 9, 32], F32, "ang")
                lmag = K.sb([128, 10, 32], F32, "lmag")
                for e_ in range(9):
                    K.op("dve", lambda e: e.tensor_scalar(out=ang[:, e_, :], in0=lidt[:, :], scalar1=float(e_), scalar2=None,
                                                          op0=ALU.mult), reads=[lidt], writes=[ang])
                    K.op("dve", lambda e: e.tensor_scalar(out=lmag[:, e_, :], in0=lrdt[:, :], scalar1=float(e_), scalar2=None,
                                                          op0=ALU.mult), reads=[lrdt], writes=[lmag])
                K.op("dve", lambda e: e.tensor_scalar(out=lmag[:, 9, :], in0=lrdt[:, :], scalar1=-16.0, scalar2=None,
                                                      op0=ALU.mult), reads=[lrdt], writes=[lmag])
                mag = K.sb([128, 10, 32], F32, "mag")
                K.op("act", lambda e: e.activation(out=mag[:, :, :], in_=lmag[:, :, :], func=AF.Exp), reads=[lmag], writes=[mag])
                sn = K.sb([128, 9, 32], F32, "sn")
                cs = K.sb([128, 9, 32], F32, "cs")
                kf = K.sb([128, 9, 32], F32, "kf")
                ki = K.sb([128, 9, 32], mybir.dt.int32, "ki")
                angp = K.sb([128, 9, 32], F32, "angp")
                for dst, shift in ((sn, 0.0), (cs, 0.5 * PI)):
                    K.op("dve", lambda e: e.tensor_scalar(out=angp[:, :, :], in0=ang[:, :, :], scalar1=shift, scalar2=None,
                                                          op0=ALU.add), reads=[ang], writes=[angp])
                    K.op("dve", lambda e: e.tensor_scalar(out=kf[:, :, :], in0=angp[:, :, :], scalar1=1.0 / (2 * PI),
                                                          scalar2=None, op0=ALU.mult), reads=[angp], writes=[kf])
                    K.op("dve", lambda e: e.tensor_copy(out=ki[:, :, :], in_=kf[:, :, :]), reads=[kf], writes=[ki])
                    K.op("dve", lambda e: e.tensor_copy(out=kf[:, :, :], in_=ki[:, :, :]), reads=[ki], writes=[kf])
                    K.op("dve", lambda e: e.scalar_tensor_tensor(out=angp[:, :, :], in0=kf[:, :, :], scalar=-2 * PI,
                                                                 in1=angp[:, :, :], op0=ALU.mult, op1=ALU.add),
                         reads=[kf, angp], writes=[angp])
                    K.op("dve", lambda e: e.tensor_scalar(out=angp[:, :, :], in0=angp[:, :, :], scalar1=-3.1415925,
                                                          scalar2=3.1415925, op0=ALU.max, op1=ALU.min),
                         reads=[angp], writes=[angp])
                    K.op("act", lambda e: e.activation(out=dst[:, :, :], in_=angp[:, :, :], func=AF.Sin),
                         reads=[angp], writes=[dst])
                Er = K.sb([128, 9, 32], F32, "Er")
                Ei = K.sb([128, 9, 32], F32, "Ei")
                tt("dve", Er[:, :, :], mag[:, 0:9, :], cs[:, :, :], ALU.mult, [mag, cs], [Er])
                tt("dve", Ei[:, :, :], mag[:, 0:9, :], sn[:, :, :], ALU.mult, [mag, sn], [Ei])
                if S5CUT == 2:
                    return
                K.op("dve", lambda e: e.tensor_copy(out=A8[:, 0, :], in_=Er[:, 8, :]), reads=[Er], writes=[A8])
                K.op("dve", lambda e: e.tensor_copy(out=A8[:, 1, :], in_=Ei[:, 8, :]), reads=[Ei], writes=[A8])
                K.op("dve", lambda e: e.tensor_scalar(out=A8[:, 2, :], in0=Ei[:, 8, :], scalar1=-1.0, scalar2=None,
                                                      op0=ALU.mult), reads=[Ei], writes=[A8])
                w_ = K.sb([128, 8, 32], F32, "zwork")
                tt("dve", w_[:, 0, :], lr[:, :], lr[:, :], ALU.mult, [lr], [w_])
                tt("dve", w_[:, 1, :], li[:, :], li[:, :], ALU.mult, [li], [w_])
                tt("dve", w_[:, 0, :], w_[:, 0, :], w_[:, 1, :], ALU.add, [w_], [w_])
                K.op("dve", lambda e: e.reciprocal(out=w_[:, 1, :], in_=w_[:, 0, :]), reads=[w_], writes=[w_])
                K.op("dve", lambda e: e.tensor_scalar(out=w_[:, 2, :], in0=Er[:, 1, :], scalar1=-1.0, scalar2=None,
                                                      op0=ALU.add), reads=[Er], writes=[w_])
                tt("dve", w_[:, 3, :], w_[:, 2, :], lr[:, :], ALU.mult, [w_, lr], [w_])
                tt("dve", w_[:, 4, :], Ei[:, 1, :], li[:, :], ALU.mult, [Ei, li], [w_])
                tt("dve", w_[:, 3, :], w_[:, 3, :], w_[:, 4, :], ALU.add, [w_], [w_])
                tt("dve", w_[:, 5, :], Ei[:, 1, :], lr[:, :], ALU.mult, [Ei, lr], [w_])
                tt("dve", w_[:, 6, :], w_[:, 2, :], li[:, :], ALU.mult, [w_, li], [w_])
                tt("dve", w_[:, 5, :], w_[:, 5, :], w_[:, 6, :], ALU.subtract, [w_], [w_])
                zr = K.sb([128, 32], F32, "zr")
                zi = K.sb([128, 32], F32, "zi")
                tt("dve", zr[:, :], w_[:, 3, :], w_[:, 1, :], ALU.mult, [w_], [zr])
                tt("dve", zi[:, :], w_[:, 5, :], w_[:, 1, :], ALU.mult, [w_], [zi])
                Gr = K.sb([128, 32], F32, "Gr")
                Gi = K.sb([128, 32], F32, "Gi")
                tt("dve", Gr[:, :], Er[:, 8, :], mag[:, 9, :], ALU.mult, [Er, mag], [Gr])
                tt("dve", Gi[:, :], A8[:, 2, :], mag[:, 9, :], ALU.mult, [A8, mag], [Gi])
                t16a = K.sb([128, 32, 16], F32, "t16a")
                t16b = K.sb([128, 32, 16], F32, "t16b")
                Bbr = K.sb([128, 32, 16], F32, "Bbr")
                Bbi = K.sb([128, 32, 16], F32, "Bbi")
                zrb = zr[:, :].unsqueeze(2).to_broadcast([128, 32, 16])
                zib = zi[:, :].unsqueeze(2).to_broadcast([128, 32, 16])
                tt("dve", t16a[:, :, :], Br[:, :, :], zrb, ALU.mult, [Br, zr], [t16a])
                tt("dve", t16b[:, :, :], Bi[:, :, :], zib, ALU.mult, [Bi, zi], [t16b])
                tt("dve", Bbr[:, :, :], t16a[:, :, :], t16b[:, :, :], ALU.subtract, [t16a, t16b], [Bbr])
                tt("dve", t16a[:, :, :], Bi[:, :, :], zrb, ALU.mult, [Bi, zr], [t16a])
                tt("dve", t16b[:, :, :], Br[:, :, :], zib, ALU.mult, [Br, zi], [t16b])
                tt("dve", Bbi[:, :, :], t16a[:, :, :], t16b[:, :, :], ALU.add, [t16a, t16b], [Bbi])
                EBr = K.sb([128, 8, 32], F32, "EBr")
                EBi = K.sb([128, 8, 32], F32, "EBi")
                ECr = K.sb([128, 8, 32], F32, "ECr")
                ECi = K.sb([128, 8, 32], F32, "ECi")
                for (EB, EC, E) in ((EBr, ECr, Er), (EBi, ECi, Ei)):
                    for j in range(8):
                        K.op("dve", lambda e: e.tensor_copy(out=EB[0:64, j, :], in_=E[0:64, 7 - j, :]), reads=[E], writes=[EB])
                        K.op("dve", lambda e: e.tensor_copy(out=EC[64:128, j, :], in_=E[64:128, 8 - j, :]), reads=[E], writes=[EC])
                    K.op("dve", lambda e: e.tensor_copy(out=EB[64:128, :, :], in_=E[64:128, 0:8, :]), reads=[E], writes=[EB])
                    K.op("dve", lambda e: e.tensor_copy(out=EC[0:64, :, :], in_=E[0:64, 1:9, :]), reads=[E], writes=[EC])
                big = [K.sb([128, 32, 8, 16], F32, "s5big%d" % i) for i in range(4)]
                BcR, BcI, T1, T2 = big
                CcR, CcI = BcR, BcI
                BcRb_t = K.sb([128, 32, 128], BF16, "BcRb")
                BcIb_t = K.sb([128, 32, 128], BF16, "BcIb")
                YRb_t = K.sb([128, 32, 128], BF16, "YRb")
                YIb_t = K.sb([128, 32, 128], BF16, "YIb")

                def outer_prod(dst, X, Et, reads):
                    tt("dve", dst[:, :, :, :], X[:, :, :].unsqueeze(2).to_broadcast([128, 32, 8, 16]),
                       Et[:, :, :].rearrange("p j g -> p g j").unsqueeze(3).to_broadcast([128, 32, 8, 16]), ALU.mult,
                       reads, [dst])

                def full(t):
                    return t[:, :, :, :]

                outer_prod(T1, Bbr, EBr, [Bbr, EBr])
                outer_prod(T2, Bbi, EBi, [Bbi, EBi])
                tt("dve", full(BcR), full(T1), full(T2), ALU.subtract, [T1, T2], [BcR])
                outer_prod(T1, Bbr, EBi, [Bbr, EBi])
                outer_prod(T2, Bbi, EBr, [Bbi, EBr])
                tt("dve", full(BcI), full(T1), full(T2), ALU.add, [T1, T2], [BcI])
                tps = [K.ps([128, 4, 128], F32, "s5tps%d" % i) for i in range(2)]
                nq = 0
                for d in range(2):
                    ps_ = slice(d * 64, (d + 1) * 64)
                    for g0 in range(0, 32, 4):
                        tp_ = tps[nq % 2]
                        nq += 1
                        for gi in range(4):
                            g = g0 + gi
                            for ri, Bx in enumerate((BcR, BcI)):
                                K.op("pe", lambda e: e.transpose(out=tp_[:, gi, ri * 64:(ri + 1) * 64],
                                                                 in_=Bx[ps_, g, :, :].rearrange("p j c -> p (j c)"),
                                                                 identity=ident_f[ps_, ps_]),
                                     reads=[Bx, ident_f], writes=[tp_], inc=(gi == 3 and ri == 1))
                        K.op("act", lambda e: e.copy(out=Bc[:, d, g0:g0 + 4, :], in_=tp_[:, :, :]), reads=[tp_], writes=[Bc])
                K.op("act", lambda e: e.copy(out=BcRb_t[:, :, :], in_=BcR[:, :, :, :].rearrange("p g t c -> p g (t c)")),
                     reads=[BcR], writes=[BcRb_t])
                K.op("act", lambda e: e.copy(out=BcIb_t[:, :, :], in_=BcI[:, :, :, :].rearrange("p g t c -> p g (t c)")),
                     reads=[BcI], writes=[BcIb_t])
                outer_prod(T1, Cr, ECr, [Cr, ECr])
                outer_prod(T2, Ci, ECi, [Ci, ECi])
                tt("dve", full(CcR), full(T1), full(T2), ALU.subtract, [T1, T2], [CcR])
                outer_prod(T1, Cr, ECi, [Cr, ECi])
                outer_prod(T2, Ci, ECr, [Ci, ECr])
                tt("dve", full(T1), full(T1), full(T2), ALU.add, [T1, T2], [T1])
                K.op("dve", lambda e: e.tensor_scalar(out=full(CcI), in0=full(T1), scalar1=-1.0, scalar2=None, op0=ALU.mult),
                     reads=[T1], writes=[CcI])
                K.op("act", lambda e: e.copy(out=CcRb[:, :, :], in_=CcR[:, :, :, :].rearrange("p g t c -> p g (t c)")),
                     reads=[CcR], writes=[CcRb])
                K.op("act", lambda e: e.copy(out=CcIb[:, :, :], in_=CcI[:, :, :, :].rearrange("p g t c -> p g (t c)")),
                     reads=[CcI], writes=[CcIb])
                if S5CUT == 3:
                    return
                Grb = Gr[:, :].unsqueeze(2).to_broadcast([128, 32, 128])
                if S5CUT == 4:
                    return
                Gib = Gi[:, :].unsqueeze(2).to_broadcast([128, 32, 128])

                def v3(t):
                    return t[:, :, :, :].rearrange("p g t c -> p g (t c)")

                tt("dve", v3(T1), v3(CcR), Grb, ALU.mult, [CcR, Gr], [T1])
                tt("dve", v3(T2), v3(CcI), Gib, ALU.mult, [CcI, Gi], [T2])
                tt("dve", v3(T1), v3(T1), v3(T2), ALU.add, [T1, T2], [T1])
                tt("dve", v3(T2), v3(CcI), Grb, ALU.mult, [CcI, Gr], [T2])
                tt("dve", v3(CcR), v3(CcR), Gib, ALU.mult, [CcR, Gi], [CcR])
                tt("dve", v3(T2), v3(T2), v3(CcR), ALU.subtract, [T2, CcR], [T2])
                YR, YI = T1, T2
                if S5CUT == 5:
                    return
                wt = [K.sb([128, 128], F32, "wt%d" % i) for i in range(2)]
                K.op("act", lambda e: e.copy(out=YRb_t[:, :, :], in_=v3(YR)), reads=[YR], writes=[YRb_t])
                K.op("act", lambda e: e.copy(out=YIb_t[:, :, :], in_=v3(YI)), reads=[YI], writes=[YIb_t])
                BcRb, BcIb, YRb, YIb = BcRb_t, BcIb_t, YRb_t, YIb_t
                for g in range(32):
                    for d in range(2):
                        ps_ = slice(d * 64, (d + 1) * 64)
                        tpd = tps[d]
                        K.op("pe", lambda e: e.matmul(out=tpd[:, g % 4, :], lhsT=BcRb[ps_, g, :], rhs=YRb[ps_, g, :],
                                                      start=True, stop=False),
                             reads=[BcRb_t, YRb_t], writes=[tpd], inc=False)
                        K.op("pe", lambda e: e.matmul(out=tpd[:, g % 4, :], lhsT=BcIb[ps_, g, :], rhs=YIb[ps_, g, :],
                                                      start=False, stop=True),
                             reads=[BcIb_t, YIb_t], writes=[tpd], inc=True)
                    if S5CUT == 7:
                        continue
                    tt("dve", wt[0][:, :], tps[0][:, g % 4, :], maskf[:, :], ALU.mult, [tps[0], maskf], [wt[0]])
                    tt("dve", wt[1][:, :], tps[1][:, g % 4, :], maskb[:, :], ALU.mult, [tps[1], maskb], [wt[1]])
                    tt("dve", wt[0][:, :], wt[0][:, :], wt[1][:, :], ALU.add, [wt[0], wt[1]], [wt[0]])
                    K.op("dve", lambda e: e.scalar_tensor_tensor(out=Wb[:, g, :], in0=ident_f[:, :], scalar=dcol[:, g:g + 1],
                                                                 in1=wt[0][:, :], op0=ALU.mult, op1=ALU.add),
                         reads=[ident_f, dcol, wt[0]], writes=[Wb])

            K.run_phase(setup)
            if "s5stop0" in phases:
                s5es.close()
                return

            def u8_factory():
                ucm = [K.sb([128, 8, 512], BF16, "ucm%d" % i) for i in range(2)]
                ugm = [K.sb([128, 32, 128], BF16, "ugm%d" % i) for i in range(2)]
                u8 = [K.sb([128, 32, 128], BF16, "u8_%d" % i) for i in range(2)]
                ups = [K.ps([128, 4, 128], F32, "u8ps%d" % i) for i in range(2)]
                cnt = {"ps": 0}

                def make_u8(sname, b, slot):
                    uc = ucm[slot]
                    K.dma("sp", uc[:, :, :], u_d[sname][b * 1024:(b + 1) * 1024, :].rearrange("(c j) ch -> c j ch", j=8),
                          writes=[uc])
                    ug = ugm[slot]
                    K.op("act", lambda e: e.copy(out=ug[:, :, :].rearrange("p g (j c) -> p g j c", c=16),
                                                 in_=uc[:, :, :].rearrange("p j (g c) -> p g j c", c=16)),
                         reads=[uc], writes=[ug])
                    u8t = u8[slot]
                    for g0 in range(0, 32, 4):
                        pt = ups[cnt["ps"] % 2]
                        cnt["ps"] += 1
                        for gi in range(4):
                            K.op("pe", lambda e: e.matmul(out=pt[:, gi, :], lhsT=ug[:, g0 + gi, :], rhs=ident_b[:, :],
                                                          start=True, stop=True), reads=[ug, ident_b], writes=[pt], inc=(gi == 3))
                        K.op("dve", lambda e: e.tensor_copy(out=u8t[:, g0:g0 + 4, :], in_=pt[:, :, :]), reads=[pt], writes=[u8t])
                    return u8t

                return make_u8

            def s5_pre():
                make_u8 = u8_factory()
                sps = [K.ps([128, 2, 2, 128], F32, "s5sps%d" % i) for i in range(2)]
                Sst = [K.sb([128, 32, 2, 128], F32, "s5Sst%d" % i) for i in range(2)]
                nb_ = 0
                nsp = 0
                for sname, L in seqs:
                    for b in range(L // 1024):
                        u8t = make_u8(sname, b, nb_ % 2)
                        st_ = Sst[nb_ % 2]
                        nb_ += 1
                        for g0 in range(0, 32, 2):
                            pt = sps[nsp % 2]
                            nsp += 1
                            for gi in range(2):
                                g = g0 + gi
                                for d in range(2):
                                    for ri in range(2):
                                        K.op("pe", lambda e: e.matmul(out=pt[d * 64:(d + 1) * 64, gi, ri, :],
                                                                      lhsT=Bc[:, d, g, ri * 64:(ri + 1) * 64],
                                                                      rhs=u8t[:, g, :], start=True, stop=True),
                                             reads=[Bc, u8t], writes=[pt], inc=(gi == 1 and d == 1 and ri == 1))
                            K.op("act", lambda e: e.copy(out=st_[:, g0:g0 + 2, :, :], in_=pt[:, :, :, :]),
                                 reads=[pt], writes=[st_])
                        K.dma("pool", S_all_d[sname][b], st_[:, :, :, :].rearrange("p g r c -> p (g r c)"), reads=[st_])

            def recurrence():
                S_buf = [K.sb([128, 32, 2, 128], F32, "s5S%d" % i) for i in range(2)]
                S_w = [[Tl(S_buf[i].t, "s5S%d_%d" % (i, d)) for d in range(2)] for i in range(2)]
                Hb16 = K.sb([128, 32, 2, 128], BF16, "s5Hb16")
                Hb16_h = [Hb16, Tl(Hb16.t, "s5Hb16_b")]
                car = K.sb([128, 32, 2], F32, "s5car")
                car_h = [car, Tl(car.t, "s5car_b")]
                tP = K.sb([128, 32, 2], F32, "s5tP")
                tP_h = [tP, Tl(tP.t, "s5tP_b")]
                tQ = K.sb([128, 32, 2], F32, "s5tQ")
                tQ_h = [tQ, Tl(tQ.t, "s5tQ_b")]
                zer = K.sb([128, 64], BF16, "s5zero")
                K.op("pool", lambda e: e.memset(zer[:, :], 0.0), writes=[zer])
                steps = [(sname, L, i) for sname, L in seqs for i in range(L // 1024)]

                def load(k):
                    sname, L, i = steps[k]
                    NB = L // 1024
                    bt = S_buf[k % 2]
                    K.dma("pool", bt[0:64, :, :, :].rearrange("p g r c -> p (g r c)"), S_all_d[sname][i, 0:64, :],
                          writes=[S_w[k % 2][0]])
                    K.dma("pool", bt[64:128, :, :, :].rearrange("p g r c -> p (g r c)"), S_all_d[sname][NB - 1 - i, 64:128, :],
                          writes=[S_w[k % 2][1]])

                load(0)
                for k, (sname, L, i) in enumerate(steps):
                    NB = L // 1024
                    NCH = L // 8
                    if i == 0:
                        K.dma("pool", Hf_d[sname][:, :, :, 0], zer[0:64, :].rearrange("p (g r) -> p g r", r=2), reads=[zer],
                              allow_slow_non_contiguous=True)
                        K.dma("pool", Hb_d[sname][:, :, :, NCH], zer[0:64, :].rearrange("p (g r) -> p g r", r=2), reads=[zer],
                              allow_slow_non_contiguous=True)
                        K.op("dve", lambda e: e.memset(car[0:64, :, :], 0.0), writes=[car_h[0]])
                        K.op("pool", lambda e: e.memset(car[64:128, :, :], 0.0), writes=[car_h[1]])
                    if k + 1 < len(steps):
                        load(k + 1)
                    S_t = S_buf[k % 2]
                    blk = (i, NB - 1 - i)
                    for d, eng in ((0, "dve"), (1, "pool")):
                        ps_ = slice(d * 64, (d + 1) * 64)
                        Sw = S_w[k % 2][d]
                        a8r = A8[ps_, 0, :].unsqueeze(2).to_broadcast([64, 32, 2])
                        order = range(128) if d == 0 else range(127, -1, -1)
                        first = True
                        for s_ in order:
                            if first:
                                prev = car[ps_, :, :]
                                prd = [car_h[d]]
                            else:
                                prev = S_t[ps_, :, :, s_ - 1 if d == 0 else s_ + 1]
                                prd = [Sw]
                            first = False
                            tt(eng, tP[ps_, :, :], prev, a8r, ALU.mult, prd + [A8], [tP_h[d]])
                            tt(eng, tQ[ps_, :, 0], prev[:, :, 1], A8[ps_, 2, :], ALU.mult, prd + [A8], [tQ_h[d]])
                            tt(eng, tQ[ps_, :, 1], prev[:, :, 0], A8[ps_, 1, :], ALU.mult, prd + [A8], [tQ_h[d]])
                            tt(eng, tP[ps_, :, :], tP[ps_, :, :], tQ[ps_, :, :], ALU.add, [tP_h[d], tQ_h[d]], [tP_h[d]])
                            tt(eng, S_t[ps_, :, :, s_], tP[ps_, :, :], S_t[ps_, :, :, s_], ALU.add, [tP_h[d], Sw], [Sw])
                        last = 127 if d == 0 else 0
                        K.op(eng, lambda e: e.tensor_copy(out=car[ps_, :, :], in_=S_t[ps_, :, :, last]),
                             reads=[Sw], writes=[car_h[d]])
                        K.op(eng, lambda e: e.tensor_copy(out=Hb16[ps_, :, :, :], in_=S_t[ps_, :, :, :]),
                             reads=[Sw], writes=[Hb16_h[d]])
                        c0 = blk[d] * 128
                        if d == 0:
                            K.dma("pool", Hf_d[sname][:, :, :, c0 + 1:c0 + 129], Hb16[0:64, :, :, :], reads=[Hb16_h[0]])
                        else:
                            K.dma("pool", Hb_d[sname][:, :, :, c0:c0 + 128], Hb16[64:128, :, :, :], reads=[Hb16_h[1]])

            def sweep2():
                make_u8 = u8_factory()
                Hl = [K.sb([128, 32, 2, 128], BF16, "s5Hl%d" % i) for i in range(2)]
                yps = [K.ps([128, 4, 128], F32, "s5yps%d" % i) for i in range(2)]
                ycm = K.sb([128, 8, 512], F32, "s5ycm")
                yg = K.sb([128, 8, 512], F32, "s5yg")
                ygb = K.sb([128, 8, 512], BF16, "s5ygb")
                so = [K.sb([128, 8, 512], BF16, "s5so%d" % i) for i in range(2)]
                gtp = tp_get()[0]
                yT = [K.sb([128, 4, 128], BF16, "s5yT%d" % i) for i in range(2)]
                gps = [K.ps([128, 512], F32, "s5gps%d" % i) for i in range(2)]
                gsb = [K.sb([128, 512], F32, "s5gsb%d" % i) for i in range(2)]
                nb2 = 0
                ny = 0
                ng = 0
                for sname, L in seqs:
                    NB = L // 1024
                    for b in range(NB):
                        u8t = make_u8(sname, b, nb2 % 2)
                        hl = Hl[nb2 % 2]
                        so_t = so[nb2 % 2]
                        nb2 += 1
                        c0 = b * 128
                        K.dma("sp", hl[0:64, :, :, :], Hf_d[sname][:, :, :, c0:c0 + 128], writes=[hl])
                        K.dma("sp", hl[64:128, :, :, :], Hb_d[sname][:, :, :, c0 + 1:c0 + 129], writes=[hl])
                        for g0 in range(0, 32, 4):
                            pt = yps[ny % 2]
                            ny += 1
                            for gi in range(4):
                                g = g0 + gi
                                K.op("pe", lambda e: e.matmul(out=pt[:, gi, :], lhsT=u8t[:, g, :], rhs=Wb[:, g, :],
                                                              start=True, stop=False), reads=[u8t, Wb], writes=[pt], inc=False)
                                K.op("pe", lambda e: e.matmul(out=pt[:, gi, :], lhsT=hl[:, g, 0, :], rhs=CcRb[:, g, :],
                                                              start=False, stop=False), reads=[hl, CcRb], writes=[pt], inc=False)
                                K.op("pe", lambda e: e.matmul(out=pt[:, gi, :], lhsT=hl[:, g, 1, :], rhs=CcIb[:, g, :],
                                                              start=False, stop=True), reads=[hl, CcIb], writes=[pt],
                                     inc=(gi == 3))
                            K.op("act", lambda e: e.copy(
                                out=ycm[:, :, g0 * 16:(g0 + 4) * 16].rearrange("p t (g c) -> p t g c", c=16),
                                in_=pt[:, :, :].rearrange("p g (t c) -> p t g c", c=16)), reads=[pt], writes=[ycm])
                        gelu_ops(ycm, yg, ygb)
                        for t_ in range(8):
                            yTt = yT[ng % 2]
                            gp = gps[ng % 2]
                            gs_ = gsb[ng % 2]
                            ng += 1
                            for k in range(4):
                                K.op("pe", lambda e: e.transpose(out=gtp[:, k, :], in_=ygb[:, t_, k * 128:(k + 1) * 128],
                                                                 identity=ident_b[:, :]), reads=[ygb, ident_b], writes=[gtp],
                                     inc=(k == 3))
                            K.op("act", lambda e: e.copy(out=yTt[:, :, :], in_=gtp[:, 0:4, :]), reads=[gtp], writes=[yTt])
                            for k in range(4):
                                K.op("pe", lambda e: e.matmul(out=gp[:, :], lhsT=yTt[:, k, :], rhs=wglu[:, k, :],
                                                              start=(k == 0), stop=(k == 3)), reads=[yTt, wglu], writes=[gp],
                                     inc=(k == 3))
                            tt("dve", gs_[:, :], gp[:, :], bglu[:, :], ALU.add, [gp, bglu], [gs_])
                            K.op("act", lambda e: e.activation(out=gs_[:, :], in_=gs_[:, :], func=AF.Sigmoid),
                                 reads=[gs_], writes=[gs_])
                            tt("pool", so_t[:, t_, :], gs_[:, :], yg[:, t_, :], ALU.mult, [gs_, yg], [so_t])
                        K.dma("pool", s5o_d[sname][b * 1024:(b + 1) * 1024, :].rearrange("(c j) ch -> c j ch", j=8),
                              so_t[:, :, :], reads=[so_t])

            def gelu_ops(ycm, yg, ygb):
                a = ycm[:, :, :]
                tt("dve", yg[:, :, :], a, a, ALU.mult, [ycm], [yg])
                K.op("dve", lambda e: e.tensor_scalar(out=yg[:, :, :], in0=yg[:, :, :], scalar1=0.044715, scalar2=1.0,
                                                      op0=ALU.mult, op1=ALU.add), reads=[yg], writes=[yg])
                tt("dve", yg[:, :, :], yg[:, :, :], a, ALU.mult, [yg, ycm], [yg])
                K.op("act", lambda e: e.activation(out=yg[:, :, :], in_=yg[:, :, :], func=AF.Sigmoid,
                                                   scale=2.0 * float(np.sqrt(2.0 / np.pi))), reads=[yg], writes=[yg])
                tt("dve", yg[:, :, :], yg[:, :, :], a, ALU.mult, [yg, ycm], [yg])
                K.op("act", lambda e: e.copy(out=ygb[:, :, :], in_=yg[:, :, :]), reads=[yg], writes=[ygb])

            K.run_phase(s5_pre)

            def attn_and_rec():
                attn_fn()
                recurrence()

            K.run_phase(attn_and_rec)
            K.run_phase(sweep2)
            s5es.close()

        nos5 = "nos5" in phases
        k0 = 4 if nos5 else 0

        def outproj_phase():
            wo = K.sb([128, 8, D], BF16, "wo")
            for k in range(8):
                K.dma("pool", wo[:, k, :], w_out[0, k * 128:(k + 1) * 128, :], writes=[wo])
            xa = [K.sb([128, D], F32, "op_x%d" % i) for i in range(4)]
            s5t = [K.sb([128, 512], BF16, "op_s%d" % i) for i in range(4)]
            catT = [K.sb([128, 8, 128], BF16, "op_c%d" % i) for i in range(4)]
            aTb = [K.sb([128, 4, 512], BF16, "op_ab%d" % i) for i in range(2)]
            dbc = [K.sb([128, 4, 512], F32, "op_db%d" % i) for i in range(2)]
            xo = [K.sb([128, D], F32, "op_xo%d" % i) for i in range(4)]
            tp = K.ps([128, 4, 128], BF16, "op_tp")
            pp = [K.ps([128, 512], F32, "op_ps%d" % i) for i in range(2)]
            otiles = [(sname, t0) for sname, L in seqs for t0 in range(0, L, 128)]
            blk_state = {}

            def stage1(n):
                sname, t0 = otiles[n]
                i2 = n % 4
                K.dma("sp", xa[i2][:], xin[sname][t0:t0 + 128, :], writes=[xa[i2]])
                if not nos5:
                    K.dma("sp", s5t[i2][:], s5o_d[sname][t0:t0 + 128, :], writes=[s5t[i2]])
                cT = catT[i2]
                ab = aTb[(t0 // 512) % 2]
                db = dbc[(t0 // 512) % 2]
                if t0 % 512 == 0:
                    K.dma("sp", ab[:, :, :], aT_d[sname][:, t0:t0 + 512].rearrange("(k p) t -> p k t", p=128),
                          writes=[ab])
                    for h in range(8):
                        K.dma("sp", db[(h % 2) * 64:(h % 2 + 1) * 64, h // 2, :],
                              den2_d[sname][h, t0:t0 + 512].partition_broadcast(64), writes=[db])
                    K.op("dve", lambda e: e.reciprocal(out=db[:, :, :], in_=db[:, :, :]), reads=[db], writes=[db])
                    K.op("dve", lambda e: e.tensor_tensor(out=ab[:, :, :], in0=ab[:, :, :], in1=db[:, :, :], op=ALU.mult),
                         reads=[ab, db], writes=[ab])
                for k in range(4 if not nos5 else 0):
                    K.op("pe", lambda e: e.transpose(out=tp[:, k, :], in_=s5t[i2][:, k * 128:(k + 1) * 128],
                                                     identity=ident_b[:, :]),
                         reads=[s5t[i2], ident_b], writes=[tp], inc=(k == 3))
                if not nos5:
                    K.op("act", lambda e: e.copy(out=cT[:, 0:4, :], in_=tp[:, :, :]), reads=[tp], writes=[cT])

            def stage2(n):
                sname, t0 = otiles[n]
                i2 = n % 4
                cT = catT[i2]
                ab = aTb[(t0 // 512) % 2]
                tsub = (t0 % 512)
                for nh in range(2):
                    for k in range(k0, 8):
                        lh = cT[:, k, :] if k < 4 else ab[:, k - 4, tsub:tsub + 128]
                        K.op("pe", lambda e: e.matmul(out=pp[nh][:, :], lhsT=lh,
                                                      rhs=wo[:, k, nh * 512:(nh + 1) * 512],
                                                      start=(k == k0), stop=(k == 7)),
                             reads=[cT, ab, wo], writes=[pp[nh]], inc=(k == 7))
                    K.op("dve", lambda e: e.tensor_tensor(out=xo[i2][:, nh * 512:(nh + 1) * 512], in0=pp[nh][:, :],
                                                          in1=xa[i2][:, nh * 512:(nh + 1) * 512], op=ALU.add),
                         reads=[pp[nh], xa[i2]], writes=[xo[i2]])
                K.dma("pool", x1[sname][t0:t0 + 128, :], xo[i2][:], reads=[xo[i2]])


            stage1(0)
            for n in range(len(otiles)):
                if n + 1 < len(otiles):
                    stage1(n + 1)
                stage2(n)

        def pool_phase(src, dst):
            pw = K.sb([128, 4, 2, 256], BF16, "pw")
            for g in range(4):
                K.dma("pool", pw[:, g, :, :], pool_w[0, g].rearrange("(k p) n -> p k n", p=128), writes=[pw])
            bd = K.sb([128, 20, 128], BF16, "bands")
            K.dma("pool", bd[:, :, :], bands_in.rearrange("w v j t -> j (w v) t"), writes=[bd])
            gam = load_rep(mix_norm[1, :], D, "gam_mix1")
            psc = load_rep(pool_scale[0, :], D, "pscale")
            xa = [K.sb([128, D], F32, "pl_x%d" % i) for i in range(6)]
            hTa = [K.sb([128, 8, 128], BF16, "pl_hT%d" % i) for i in range(3)]
            zt = [K.sb([128, D], BF16, "pl_z%d" % i) for i in range(4)]
            zp = [K.ps([128, 512], F32, "pl_zp%d" % i) for i in range(2)]
            op_ = [K.ps([128, 512], F32, "pl_op%d" % i) for i in range(2)]
            tmp = K.sb([128, D], F32, "pl_tmp")
            xo = [K.sb([128, D], F32, "pl_xo%d" % i) for i in range(3)]
            nn = 0
            for sname, L in seqs:
                NTt = L // 128

                def make_z(i):
                    x_t = xa[i % 6]
                    K.dma("sp", x_t[:], src[sname][i * 128:(i + 1) * 128, :], writes=[x_t])
                    hT_t = hTa[i % 3]
                    norm_to_T(x_t, 128, gam, hT_t, hT_t[:, :, :])
                    for g in range(4):
                        for k in range(2):
                            K.op("pe", lambda e: e.matmul(out=zp[g // 2][:, (g % 2) * 256:(g % 2) * 256 + 256],
                                                          lhsT=hT_t[:, 2 * g + k, :], rhs=pw[:, g, k, :],
                                                          start=(k == 0), stop=(k == 1)),
                                 reads=[hT_t, pw], writes=[zp[g // 2]], inc=(k == 1))
                    for hf in range(2):
                        K.op("act", lambda e: e.copy(out=zt[i % 4][:, hf * 512:(hf + 1) * 512], in_=zp[hf][:, :]),
                             reads=[zp[hf]], writes=[zt[i % 4]])

                make_z(0)
                make_z(1)
                for i in range(NTt):
                    if i + 2 < NTt:
                        make_z(i + 2)
                    for g in range(4):
                        terms = []
                        if i > 0:
                            terms.append((zt[(i - 1) % 4], 0))
                        terms.append((zt[i % 4], 3 if i == 0 else (4 if i == NTt - 1 else 2)))
                        if i + 1 < NTt:
                            terms.append((zt[(i + 1) % 4], 1))
                        for ti, (zsrc, var) in enumerate(terms):
                            K.op("pe", lambda e: e.matmul(out=op_[g // 2][:, (g % 2) * 256:(g % 2) * 256 + 256],
                                                          lhsT=bd[:, g * 5 + var, :], rhs=zsrc[:, g * 256:(g + 1) * 256],
                                                          start=(ti == 0), stop=(ti == len(terms) - 1)),
                                 reads=[bd, zsrc], writes=[op_[g // 2]], inc=(ti == len(terms) - 1))
                    xo_t = xo[nn % 3]
                    nn += 1
                    for hf in range(2):
                        K.op("dve", lambda e: e.tensor_tensor(out=tmp[:, hf * 512:(hf + 1) * 512], in0=op_[hf][:, :],
                                                              in1=psc[:, hf * 512:(hf + 1) * 512], op=ALU.mult),
                             reads=[op_[hf], psc], writes=[tmp])
                    K.op("dve", lambda e: e.tensor_tensor(out=xo_t[:, :], in0=tmp[:, :], in1=xa[i % 6][:, :], op=ALU.add),
                         reads=[tmp, xa[i % 6]], writes=[xo_t])
                    K.dma("pool", dst[sname][i * 128:(i + 1) * 128, :], xo_t[:], reads=[xo_t])

        if "ffn_only" in phases:
            K.run_phase(lambda: ffn_phase(0, xin, yout, True))
        if "full" in phases:
            K.run_phase(proj_phase)
            if "nos5" not in phases:
                s5_phase(attn_phase)
            else:
                K.run_phase(attn_phase)
            K.run_phase(outproj_phase)
            K.run_phase(lambda: ffn_phase(0, x1, x2, False))
            K.run_phase(lambda: pool_phase(x2, x3))
            K.run_phase(lambda: ffn_phase(1, x3, yout, True))
    return nc


PHASES = ("full",)


def kernel(**inputs):
    x_prompt = np.ascontiguousarray(np.asarray(inputs["x_prompt"], dtype=np.float32))
    x_sample = np.ascontiguousarray(np.asarray(inputs["x_sample"], dtype=np.float32))
    n = x_prompt.shape[0]
    LP, LS = x_prompt.shape[1], x_sample.shape[1]
    nc = bass.Bass("TRN2", target_bir_lowering=False)
    build_program(nc, LP, LS, phases=PHASES)
    shared = {k: np.ascontiguousarray(np.asarray(v, dtype=np.float32)) for k, v in inputs.items()
              if k not in ("x_prompt", "x_sample")}
    shared.update(host_consts(LP))
    in_maps = []
    for i in range(n):
        m = dict(shared)
        m["x_p"] = x_prompt[i]
        m["x_s"] = x_sample[i]
        in_maps.append(m)
    res = run_bass_kernel_spmd(nc, in_maps, core_ids=list(range(n)))
    y_p = np.stack([np.asarray(r["y_p"], dtype=np.float32) for r in res.results], 0)
    y_s = np.stack([np.asarray(r["y_s"], dtype=np.float32) for r in res.results], 0)
    return (y_p, y_s)


def _rope_tables(L):
    t = np.arange(L)
    row = (t // 64).astype(np.float32)
    col = (t % 64).astype(np.float32)
    inv = np.power(np.float32(10000.0), -np.arange(16, dtype=np.float32) / np.float32(16)).astype(np.float32)
    ar = (row[:, None] * inv).astype(np.float32)
    ac = (col[:, None] * inv).astype(np.float32)
    cos = np.stack([np.cos(ar), np.cos(ac)], 1).astype(np.float32)
    sin = np.stack([np.sin(ar), np.sin(ac)], 1).astype(np.float32)
    cos10 = np.ascontiguousarray(np.broadcast_to(cos[:, None], (L, 10, 2, 16)).reshape(L, 320))
    sin10 = np.ascontiguousarray(np.broadcast_to(sin[:, None], (L, 10, 2, 16)).reshape(L, 320))
    return cos10, sin10


def _bands():
    out = np.zeros((4, 5, 128, 128), np.float32)
    L = 384
    for g, w in enumerate((2, 4, 8, 16)):
        t = np.arange(L)
        lo = np.maximum(t - w // 2, 0)
        hi = np.minimum(t + (w - w // 2) - 1, L - 1)
        cnt = (hi - lo + 1).astype(np.float64)
        j = np.arange(L)[:, None]
        M = ((j >= lo[None, :]) & (j <= hi[None, :])) / cnt[None, :] - np.eye(L)
        out[g, 0] = M[0:128, 128:256]
        out[g, 1] = M[256:384, 128:256]
        out[g, 2] = M[128:256, 128:256]
        out[g, 3] = M[0:128, 0:128]
        out[g, 4] = M[256:384, 256:384]
    return out


def host_consts(LP):
    c, s = _rope_tables(LP)
    jj = np.arange(128)[:, None] // 16
    tt_ = np.arange(128)[None, :] // 16
    return dict(ident=np.eye(128, dtype=np.float32), rope_cos=c, rope_sin=s, bands=_bands(),
                maskf=(tt_ >= jj).astype(np.float32), maskb=(tt_ <= jj).astype(np.float32))
```
